# Optimizing a Trainium2 kernel written in Bass

```python
import math
import jax, jax.numpy as jnp
from jax import lax
import numpy as np

D_MODEL = 2048
BATCH = 4
SEQ = 2048
DEPTH = 1

CHUNK = 64
N_META = 16
D_SSM = D_MODEL // 2
SSM_GROUP = 16
N_SSM_GROUPS = D_SSM // SSM_GROUP
SSM_STATE = 64
D_CONV = D_MODEL // 2
CONV_WIDTH = 31
D_FF = 4 * D_MODEL
EPS = 1e-6
DT_MIN = 1e-3
DT_MAX = 1e-1

kernel_name = "s5_conformer_gated_hybrid"


def rms_norm(x, g):
    xf = x.astype(jnp.float32)
    y = xf * lax.rsqrt(jnp.mean(xf * xf, axis=-1, keepdims=True) + EPS)
    return (y * g.astype(jnp.float32)).astype(x.dtype)


def layer_norm(x, g, b):
    xf = x.astype(jnp.float32)
    mu = jnp.mean(xf, axis=-1, keepdims=True)
    xc = xf - mu
    y = xc * lax.rsqrt(jnp.mean(xc * xc, axis=-1, keepdims=True) + EPS)
    return (y * g.astype(jnp.float32) + b.astype(jnp.float32)).astype(x.dtype)


def s5_mixer(u, lam_re, lam_im, log_dt, b_re, b_im, c_re, c_im, d_skip):
    bsz, seq_len, _ = u.shape
    uf = u.astype(jnp.float32)
    ug = uf.reshape(bsz, seq_len, N_SSM_GROUPS, SSM_GROUP)
    lam = lax.complex(lam_re.astype(jnp.float32), lam_im.astype(jnp.float32))
    dt = jnp.exp(log_dt.astype(jnp.float32))[:, None]
    lam_bar = jnp.exp(lam * dt)
    b_cplx = lax.complex(b_re.astype(jnp.float32), b_im.astype(jnp.float32))
    b_bar = ((lam_bar - 1.0) / lam)[..., None] * b_cplx
    bu = jnp.einsum('gnh,blgh->blgn', b_bar, ug.astype(jnp.complex64))
    a = jnp.broadcast_to(lam_bar, bu.shape)

    def combine(e1, e2):
        a1, s1 = e1
        a2, s2 = e2
        return a1 * a2, a2 * s1 + s2

    _, states = lax.associative_scan(combine, (a, bu), axis=1)
    y = (jnp.einsum('ghn,blgn->blgh', c_re.astype(jnp.float32), states.real)
         - jnp.einsum('ghn,blgn->blgh', c_im.astype(jnp.float32), states.imag))
    return y.reshape(bsz, seq_len, D_SSM) + d_skip.astype(jnp.float32) * uf


def causal_depthwise_conv(x, w, b):
    k = w.shape[0]
    xp = jnp.pad(x, ((0, 0), (k - 1, 0), (0, 0)))
    y = lax.conv_general_dilated(xp, w[:, None, :].astype(x.dtype), window_strides=(1,),
                                 padding='VALID', dimension_numbers=('NWC', 'WIO', 'NWC'),
                                 feature_group_count=x.shape[-1])
    return y + b.astype(x.dtype)


def setup_inputs(seed: int = 0) -> dict:
    key = jax.random.key(seed)
    ks = jax.random.split(key, 24)
    f32 = jnp.float32
    n_in = D_SSM + 2 * D_CONV + 2 * D_MODEL
    nrm = lambda k, shape, scale: (jax.random.normal(k, shape, f32) * scale).astype(f32)
    x = jax.random.normal(ks[0], (BATCH, SEQ, D_MODEL), f32)
    meta = nrm(ks[1], (N_META, D_MODEL), 1.0)
    norm_mix_g = 1.0 + nrm(ks[2], (DEPTH, D_MODEL), 0.02)
    w_in = nrm(ks[3], (DEPTH, D_MODEL, n_in), D_MODEL ** -0.5)
    lam_re = -0.5 + nrm(ks[4], (DEPTH, N_SSM_GROUPS, SSM_STATE), 0.01)
    lam_im = (math.pi * jnp.arange(SSM_STATE, dtype=f32))[None, None, :] + nrm(ks[5], (DEPTH, N_SSM_GROUPS, SSM_STATE), 0.01)
    log_dt = jax.random.uniform(ks[6], (DEPTH, N_SSM_GROUPS), f32, math.log(DT_MIN), math.log(DT_MAX))
    b_re = nrm(ks[7], (DEPTH, N_SSM_GROUPS, SSM_STATE, SSM_GROUP), (2 * SSM_GROUP) ** -0.5)
    b_im = nrm(ks[8], (DEPTH, N_SSM_GROUPS, SSM_STATE, SSM_GROUP), (2 * SSM_GROUP) ** -0.5)
    c_re = nrm(ks[9], (DEPTH, N_SSM_GROUPS, SSM_GROUP, SSM_STATE), (2 * SSM_STATE) ** -0.5)
    c_im = nrm(ks[10], (DEPTH, N_SSM_GROUPS, SSM_GROUP, SSM_STATE), (2 * SSM_STATE) ** -0.5)
    d_skip = nrm(ks[11], (DEPTH, D_SSM), 1.0)
    w_glu = nrm(ks[12], (DEPTH, D_SSM, 2 * D_MODEL), D_SSM ** -0.5)
    conv_w = nrm(ks[13], (DEPTH, CONV_WIDTH, D_CONV), CONV_WIDTH ** -0.5)
    conv_b = nrm(ks[14], (DEPTH, D_CONV), 0.01)
    conv_ln_g = 1.0 + nrm(ks[15], (DEPTH, D_CONV), 0.02)
    conv_ln_b = nrm(ks[16], (DEPTH, D_CONV), 0.01)
    w_conv_out = nrm(ks[17], (DEPTH, D_CONV, D_MODEL), D_CONV ** -0.5)
    w_out = nrm(ks[18], (DEPTH, D_MODEL, D_MODEL), D_MODEL ** -0.5)
    norm_ffn_g = 1.0 + nrm(ks[19], (DEPTH, D_MODEL), 0.02)
    w_ff1 = nrm(ks[20], (DEPTH, D_MODEL, D_FF), D_MODEL ** -0.5)
    w_ff2 = nrm(ks[21], (DEPTH, D_FF, D_MODEL), D_FF ** -0.5)
    norm_f_g = 1.0 + nrm(ks[22], (D_MODEL,), 0.02)
    return {"x": x, "meta": meta, "norm_mix_g": norm_mix_g, "w_in": w_in,
            "lam_re": lam_re, "lam_im": lam_im, "log_dt": log_dt,
            "b_re": b_re, "b_im": b_im, "c_re": c_re, "c_im": c_im, "d_skip": d_skip,
            "w_glu": w_glu, "conv_w": conv_w, "conv_b": conv_b,
            "conv_ln_g": conv_ln_g, "conv_ln_b": conv_ln_b, "w_conv_out": w_conv_out,
            "w_out": w_out, "norm_ffn_g": norm_ffn_g, "w_ff1": w_ff1, "w_ff2": w_ff2,
            "norm_f_g": norm_f_g}


def reference(x, meta, norm_mix_g, w_in, lam_re, lam_im, log_dt, b_re, b_im, c_re, c_im, d_skip,
              w_glu, conv_w, conv_b, conv_ln_g, conv_ln_b, w_conv_out, w_out, norm_ffn_g,
              w_ff1, w_ff2, norm_f_g):
    bsz = x.shape[0]
    h_res = jnp.concatenate([jnp.broadcast_to(meta.astype(x.dtype)[None], (bsz, N_META, D_MODEL)), x], axis=1)
    split_pts = [D_SSM, D_SSM + 2 * D_CONV, D_SSM + 2 * D_CONV + D_MODEL]
    for l in range(DEPTH):
        h = rms_norm(h_res, norm_mix_g[l])
        proj = h @ w_in[l]
        u, cv, ga, gb = jnp.split(proj, split_pts, axis=-1)
        y = s5_mixer(u, lam_re[l], lam_im[l], log_dt[l], b_re[l], b_im[l], c_re[l], c_im[l], d_skip[l])
        z = jax.nn.gelu(y).astype(x.dtype)
        za, zb = jnp.split(z @ w_glu[l], 2, axis=-1)
        out_a = za * jax.nn.sigmoid(zb)
        c1, c2 = jnp.split(cv, 2, axis=-1)
        c = c1 * jax.nn.sigmoid(c2)
        c = causal_depthwise_conv(c, conv_w[l], conv_b[l])
        c = jax.nn.silu(layer_norm(c, conv_ln_g[l], conv_ln_b[l]))
        out_b = c @ w_conv_out[l]
        merged = jax.nn.sigmoid(ga) * out_a + jax.nn.sigmoid(gb) * out_b
        h_res = h_res + merged @ w_out[l]
        h = rms_norm(h_res, norm_ffn_g[l])
        h_res = h_res + jnp.square(jax.nn.relu(h @ w_ff1[l])) @ w_ff2[l]
    out = rms_norm(h_res, norm_f_g)
    return out[:, N_META:, :]
```

```python
import math
from contextlib import ExitStack
import numpy as np
import concourse.bass as bass
import concourse.mybir as mybir
from concourse.bass_utils import run_bass_kernel_spmd

F32 = mybir.dt.float32
BF16 = mybir.dt.bfloat16
I32 = mybir.dt.int32
AF = mybir.ActivationFunctionType
ALU = mybir.AluOpType

D = 2048
NP = 1040
NM = 1024
NCP = NP // 8
NCM = NM // 8
EPS = 1e-6
TWO_PI = 2.0 * math.pi
ARENA_BYTES = 200 * 1024
DEBUG = False
SAME_ENGINE_MIN_DIST = 10 ** 9


class Op:
    __slots__ = ("eng", "fn", "deps", "semkey", "seq", "val", "need", "is_dma", "is_mm")


class Sched:
    ENGS = ("pe", "act", "dve", "pool", "sp")

    GROUPS = (("w", 8), ("x", 3), ("y", 2), ("misc", 12), ("dbg", 2))

    def __init__(self):
        self.ops = {e: [] for e in self.ENGS}
        self.tiles = {}
        self.buf_fence = {}
        self.grp = {}
        base = 0
        for name, n in self.GROUPS:
            self.grp[name] = [base, n, 0]
            base += n
        self.n_dma_sems = base
        self.dma_last = [None] * base
        self.dma_count = [0] * base
        self.all_ops = []

    @staticmethod
    def _merge(dst, src):
        for k, o in src.items():
            cur = dst.get(k)
            if cur is None or o.seq > cur.seq:
                dst[k] = o

    def _tile(self, key):
        t = self.tiles.get(key)
        if t is None:
            f = self.buf_fence.get(key[0]) if isinstance(key, tuple) else None
            t = [dict(f) if f else {}, {}]
            self.tiles[key] = t
        return t

    def _add(self, eng, fn, r, w, is_dma=False, is_mm=False, group="misc", slot=None):
        op = Op()
        op.eng = eng
        op.fn = fn
        op.is_dma = is_dma
        op.is_mm = is_mm
        op.need = is_dma
        op.val = None
        deps = {}
        for k in r:
            self._merge(deps, self._tile(k)[0])
        for k in w:
            t = self._tile(k)
            self._merge(deps, t[0])
            self._merge(deps, t[1])
        if is_dma:
            gi = self.grp[group]
            if slot is None:
                slot = gi[2]
                gi[2] = (gi[2] + 1) % gi[1]
            s = gi[0] + (slot % gi[1])
            prev = self.dma_last[s]
            if prev is not None:
                self._merge(deps, {prev.semkey: prev})
            self.dma_count[s] += 1
            op.semkey = ("dma", s)
            op.seq = self.dma_count[s]
            op.val = 16 * self.dma_count[s]
            self.dma_last[s] = op
        else:
            op.semkey = eng
            op.seq = len(self.ops[eng])
        if eng == "pe" and "pe" in deps:
            del deps["pe"]
        if (not is_dma) and eng in ("dve", "act") and eng in deps and len(self.ops[eng]) - deps[eng].seq >= SAME_ENGINE_MIN_DIST:
            del deps[eng]
        op.deps = list(deps.values())
        for d in op.deps:
            d.need = True
        me = {op.semkey: op}
        for k in w:
            t = self._tile(k)
            t[0] = dict(me)
            t[1] = {}
        for k in r:
            self._merge(self._tile(k)[1], me)
        self.ops[eng].append(op)
        self.all_ops.append(op)
        return op

    def op(self, eng, fn, r=(), w=(), mm=False):
        return self._add(eng, fn, r, w, is_mm=mm)

    def dma(self, eng, fn, r=(), w=(), group="misc", slot=None):
        return self._add(eng, fn, r, w, is_dma=True, group=group, slot=slot)

    def retire(self, old_names, new_name):
        f = self.buf_fence.setdefault(new_name, {})
        for key, t in self.tiles.items():
            if isinstance(key, tuple) and key[0] in old_names:
                self._merge(f, t[0])
                self._merge(f, t[1])
        for n in old_names:
            if n in self.buf_fence and n != new_name:
                self._merge(f, self.buf_fence[n])
        for key, t in self.tiles.items():
            if isinstance(key, tuple) and key[0] == new_name:
                self._merge(t[0], f)

    def finalize(self):
        for e in self.ENGS:
            c = 0
            for o in self.ops[e]:
                if not o.is_dma:
                    if o.need:
                        c += 1
                        o.val = c

    def emit(self, eng_name, eng, sems, dma_sems, final_wait_all_dma=False):
        seen = {}
        for o in self.ops[eng_name]:
            for d in o.deps:
                k = d.semkey
                if seen.get(k, 0) >= d.val:
                    continue
                seen[k] = d.val
                sem = dma_sems[k[1]] if isinstance(k, tuple) else sems[k]
                eng.wait_ge(sem, d.val)
            ins = o.fn(eng)
            if o.is_dma:
                ins.then_inc(dma_sems[o.semkey[1]], 16)
            elif o.need:
                ins.then_inc(sems[o.semkey], 1)
        if final_wait_all_dma:
            for s in range(self.n_dma_sems):
                if self.dma_count[s] > 0:
                    eng.wait_ge(dma_sems[s], 16 * self.dma_count[s])


class Mem:
    def __init__(self, arena, nbytes, sched):
        self.arena = arena
        self.nbytes = nbytes
        self.live = {}
        self.dead = []
        self.S = sched

    def alloc(self, name, shape, dt):
        n = int(np.prod(shape[1:]))
        esz = 4 if dt in (F32, I32) else 2
        nb = (n * esz + 63) // 64 * 64
        ivs = sorted(self.live.values())
        lo = 0
        for (a, b) in ivs:
            if a - lo >= nb:
                break
            lo = max(lo, b)
        assert lo + nb <= self.nbytes, f"SBUF arena overflow allocating {name} ({nb} B); live={self.live}"
        hi = lo + nb
        self.live[name] = (lo, hi)
        self.all_names = getattr(self, "all_names", set())
        self.all_names.add(name)
        olds = [nm for (a, b, nm) in self.dead if a < hi and b > lo]
        if olds:
            self.S.retire(set(olds), name)
        v = self.arena[0:shape[0], lo // 4:(lo + n * esz + 3) // 4]
        if dt != F32:
            v = v.bitcast(dt)
        v = v[:, 0:n]
        if len(shape) == 3:
            v = v.rearrange("p (a b) -> p a b", a=shape[1])
        elif len(shape) == 4:
            v = v.rearrange("p (a b c) -> p a b c", a=shape[1], b=shape[2])
        return v

    def free(self, *names):
        for name in names:
            lo, hi = self.live.pop(name)
            self.dead.append((lo, hi, name))


def build_program():
    nc = bass.Bass("TRN2", target_bir_lowering=False)

    def din(name, shape):
        return nc.dram_tensor(name, list(shape), F32, kind="ExternalInput").ap()

    xp_d = din("xp", [NP, D])
    xm_d = din("xm", [NM, D])
    w_in = din("w_in", [D, 7168])
    w_glu = din("w_glu", [1024, 4096])
    w_co = din("w_conv_out", [1024, 2048])
    w_out = din("w_out", [D, D])
    w_ff1 = din("w_ff1", [D, 8192])
    w_ff2 = din("w_ff2", [8192, D])
    lam_re_d = din("lam_re", [64, 64])
    lam_im_d = din("lam_im", [64, 64])
    log_dt_d = din("log_dt", [64])
    b_re_d = din("b_re", [64, 64, 16])
    b_im_d = din("b_im", [64, 64, 16])
    c_re_d = din("c_re", [64, 16, 64])
    c_im_d = din("c_im", [64, 16, 64])
    d_skip_d = din("d_skip", [1024])
    conv_w_d = din("conv_w", [31, 1024])
    conv_b_d = din("conv_b", [1024])
    ln_g_d = din("conv_ln_g", [1024])
    ln_b_d = din("conv_ln_b", [1024])
    g_mix_d = din("norm_mix_g", [D])
    g_ffn_d = din("norm_ffn_g", [D])
    g_f_d = din("norm_f_g", [D])
    ident_d = din("c_ident", [128, 128])
    sel_d = din("c_sel", [128, 64 * 128])
    mask_d = din("c_mask", [128, 128])
    kvec_d = din("c_kvec", [24])
    qvec_d = din("c_qvec", [NCP])
    y_d = nc.dram_tensor("y", [NM, D], F32, kind="ExternalOutput").ap()

    S = Sched()
    es = ExitStack()
    dbg_outs = {}

    def dump(name, ap, r):
        if not DEBUG:
            return
        shp = list(ap.shape)
        dt = ap.dtype
        d = nc.dram_tensor("dbg_" + name, shp, dt, kind="ExternalOutput").ap()
        dbg_outs[name] = d
        S.dma("sp", lambda e: e.dma_start(out=d, in_=ap), r=r, w=[("dbg", name)], group="dbg")

    arena = es.enter_context(nc.sbuf_tensor("arena", [128, ARENA_BYTES // 4], F32))
    M = Mem(arena, ARENA_BYTES, S)
    banks = [es.enter_context(nc.psum_tensor(f"psb{i}", [128, 512], F32)) for i in range(8)]
    bank_rr = [0]

    def psum(which=None):
        if which is None:
            which = bank_rr[0]
            bank_rr[0] = (bank_rr[0] + 1) % 8
        return banks[which], ("psum", which)

    def psum_from(lst, ctr):
        b = lst[ctr[0] % len(lst)]
        ctr[0] += 1
        return banks[b], ("psum", b)

    def mm(out, lhsT, rhs, start, stop, r, w):
        S.op("pe", lambda e: e.matmul(out, lhsT=lhsT, rhs=rhs, start=start, stop=stop), r=r, w=w, mm=True)

    def tr(out, in_, ident, r, w):
        S.op("pe", lambda e: e.transpose(out, in_, ident), r=r, w=w, mm=True)

    def act(out, in_, func, r, w, scale=1.0, bias=None, accum=None):
        def fn(e):
            kw = {}
            if bias is not None:
                kw["bias"] = bias
            if accum is not None:
                kw["accum_out"] = accum
            return e.activation(out=out, in_=in_, func=func, scale=scale, **kw)
        S.op("act", fn, r=r, w=w)

    def tt(eng, out, in0, in1, op, r, w):
        S.op(eng, lambda e: e.tensor_tensor(out=out, in0=in0, in1=in1, op=op), r=r, w=w)

    def ts(eng, out, in0, s1, s2, op0, op1, r, w):
        if op1 is None:
            S.op(eng, lambda e: e.tensor_scalar(out=out, in0=in0, scalar1=s1, scalar2=None, op0=op0), r=r, w=w)
        else:
            S.op(eng, lambda e: e.tensor_scalar(out=out, in0=in0, scalar1=s1, scalar2=s2, op0=op0, op1=op1), r=r, w=w)

    def stt(out, in0, scalar, in1, op0, op1, r, w):
        S.op("dve", lambda e: e.scalar_tensor_tensor(out=out, in0=in0, scalar=scalar, in1=in1, op0=op0, op1=op1), r=r, w=w)

    def cp(eng, out, in_, r, w):
        if eng == "act":
            act(out, in_, AF.Copy, r, w)
        else:
            S.op(eng, lambda e: e.tensor_copy(out=out, in_=in_), r=r, w=w)

    def dma(eng, out, in_, r, w, slow=False, group="misc", slot=None):
        if slow:
            S.dma(eng, lambda e: e.dma_start(out=out, in_=in_, allow_slow_non_contiguous=True), r=r, w=w, group=group, slot=slot)
        else:
            S.dma(eng, lambda e: e.dma_start(out=out, in_=in_), r=r, w=w, group=group, slot=slot)

    def memset(eng, ap, val, w):
        S.op(eng, lambda e: e.memset(ap, val), r=(), w=w)

    evac_rr = [0]

    def evac_eng():
        evac_rr[0] += 1
        return "act" if evac_rr[0] % 2 else "dve"

    identf = M.alloc("identf", [128, 128], F32)
    identb = M.alloc("identb", [128, 128], BF16)
    onesb = M.alloc("onesb", [128, 128], BF16)
    sel = M.alloc("sel", [128, 64, 128], BF16)
    gmix_bc = M.alloc("gmix_bc", [128, D], F32)
    gffn = M.alloc("gffn", [128, 16], F32)
    convb = M.alloc("convb", [128, 8], F32)
    lng = M.alloc("lng", [128, 8], F32)
    lnb = M.alloc("lnb", [128, 8], F32)
    convw = M.alloc("convw", [128, 8, 31], F32)

    dma("sp", identf, ident_d, r=(), w=[("identf",)])
    cp("dve", identb, identf, r=[("identf",)], w=[("identb",)])
    memset("dve", onesb, 1.0, w=[("onesb",)])
    def late_loads():
        dma("pool", sel, sel_d.rearrange("p (a b) -> p a b", b=128), r=(), w=[("sel",)], group="w", slot=7)
        dma("sp", gmix_bc, g_mix_d.partition_broadcast(128), r=(), w=[("gmix_bc",)])
        dma("act", gffn, g_ffn_d.rearrange("(c p) -> p c", p=128), r=(), w=[("gffn",)], slow=True)
        dma("act", convb, conv_b_d.rearrange("(c p) -> p c", p=128), r=(), w=[("convb",)], slow=True)
        dma("act", lng, ln_g_d.rearrange("(c p) -> p c", p=128), r=(), w=[("lng",)], slow=True)
        dma("act", lnb, ln_b_d.rearrange("(c p) -> p c", p=128), r=(), w=[("lnb",)], slow=True)

        cw_raw = M.alloc("cw_raw", [31, 1024], F32)
        dma("sp", cw_raw, conv_w_d, r=(), w=[("cw_raw",)])
        for half in range(2):
            ps, pk = psum()
            for jj in range(4):
                j = half * 4 + jj
                tr(ps[:, jj * 32:jj * 32 + 31], cw_raw[0:31, j * 128:(j + 1) * 128], identf[0:31, 0:31],
                   r=[("cw_raw",), ("identf",)], w=[pk])
            cp("dve", convw[:, half * 4:half * 4 + 4, :],
               ps[:, 0:128].rearrange("p (a b) -> p a b", a=4)[:, :, 0:31], r=[pk], w=[("convw",)])
        M.free("cw_raw")

    lamre = M.alloc("lamre", [128, 32], F32)
    lamim = M.alloc("lamim", [128, 32], F32)
    dtt = M.alloc("dtt", [128, 32], F32)
    bre = M.alloc("bre", [128, 32, 16], F32)
    bim = M.alloc("bim", [128, 32, 16], F32)
    cre = M.alloc("cre", [128, 32, 16], F32)
    cim = M.alloc("cim", [128, 32, 16], F32)
    kv = M.alloc("kv", [128, 24], F32)
    dsk = M.alloc("dsk", [128, 64], F32)
    maskt = M.alloc("maskt", [128, 128], F32)
    phiw = M.alloc("phiw", [128, 32], F32)
    qv = M.alloc("qv", [128, NCP], F32)
    rho = M.alloc("rho", [128, 32], F32)
    fcos = M.alloc("fcos", [128, 32], F32)
    fsin = M.alloc("fsin", [128, 32], F32)
    ZW = M.alloc("ZW", [128, 32, 2, 128], BF16)
    KW = M.alloc("KW", [128, 64, 128], BF16)
    OW = M.alloc("OW", [128, 32, 2, 128], BF16)

    lam_raw = M.alloc("lam_raw", [32, 3, 128], F32)
    ldt = M.alloc("ldt", [32, 2], F32)
    dma("sp", lam_raw[:, 0, :], lam_re_d.rearrange("(p g2) n -> p (g2 n)", g2=2), r=(), w=[("lam_raw", 0)])
    dma("act", lam_raw[:, 1, :], lam_im_d.rearrange("(p g2) n -> p (g2 n)", g2=2), r=(), w=[("lam_raw", 1)])
    dma("sp", ldt, log_dt_d.rearrange("(p g2) -> p g2", g2=2), r=(), w=[("ldt",)])
    dma("sp", kv, kvec_d.partition_broadcast(128), r=(), w=[("kv",)])
    cp("dve", lam_raw[:, 2, :].rearrange("p (g2 n) -> p g2 n", g2=2), ldt.unsqueeze(2).to_broadcast([32, 2, 64]),
       r=[("ldt",)], w=[("lam_raw", 2)])
    ps, pk = psum()
    for w_ in range(3):
        tr(ps[:, 32 * w_:32 * w_ + 32], lam_raw[0:32, w_, :], identf[0:32, 0:32], r=[("lam_raw", w_), ("identf",)], w=[pk])
    cp("dve", lamre, ps[:, 0:32], r=[pk], w=[("lamre",)])
    cp("dve", lamim, ps[:, 32:64], r=[pk], w=[("lamim",)])
    cp("dve", dtt, ps[:, 64:96], r=[pk], w=[("dtt", 0), ("dtt", 1)])
    M.free("lam_raw", "ldt")
    dma("sp", bre, b_re_d.rearrange("(p g2) n h -> (g2 n) p h", g2=2), r=(), w=[("bre",)])
    dma("act", bim, b_im_d.rearrange("(p g2) n h -> (g2 n) p h", g2=2), r=(), w=[("bim",)])
    for nm, src, dst in (("cre", c_re_d, cre), ("cim", c_im_d, cim)):
        raw = M.alloc(nm + "_raw", [64, 8, 128], F32)
        sv = src.rearrange("(b q g2) h n -> q h b g2 n", q=4, g2=2)
        for q4 in range(4):
            for g2 in range(2):
                dma("sp" if g2 == 0 else "act", raw[16 * q4:16 * q4 + 16, :, 64 * g2:64 * g2 + 64], sv[q4][:, :, g2, :],
                    r=(), w=[(nm + "_raw", q4, g2)])
        for b4 in range(2):
            ps, pk = psum()
            for bb in range(4):
                b = b4 * 4 + bb
                tr(ps[:, bb * 64:(bb + 1) * 64], raw[0:64, b, :], identf[0:64, 0:64],
                   r=[(nm + "_raw", q, g2) for q in range(4) for g2 in range(2)] + [("identf",)], w=[pk])
            cp("dve", dst[:, 16 * b4:16 * b4 + 16, :].rearrange("p a h -> p (a h)"), ps[:, 0:256], r=[pk], w=[(nm,)])
        M.free(nm + "_raw")

    dma("act", qv, qvec_d.partition_broadcast(128), r=(), w=[("qv",)])
    dma("act", maskt, mask_d, r=(), w=[("maskt",)])
    dsv = d_skip_d.rearrange("(g ch) -> ch g", ch=16)
    for i in range(8):
        dma("sp" if i % 2 == 0 else "act", dsk[16 * i:16 * i + 16, :], dsv, r=(), w=[("dsk", i)], slow=True)

    a_t = M.alloc("a_t", [128, 32], F32)
    th_t = M.alloc("th_t", [128, 32], F32)
    mag = M.alloc("mag", [128, 32, 24], F32)
    ang = M.alloc("ang", [128, 32, 24], F32)
    angi = M.alloc("angi", [128, 32, 24], I32)
    angf = M.alloc("angf", [128, 32, 24], F32)
    cosv = M.alloc("cosv", [128, 32, 24], F32)
    sinv = M.alloc("sinv", [128, 32, 24], F32)
    LPre = M.alloc("LPre", [128, 32, 24], F32)
    LPim = M.alloc("LPim", [128, 32, 24], F32)

    act(dtt, dtt, AF.Exp, r=[("dtt", 0), ("dtt", 1)], w=[("dtt", 0), ("dtt", 1)])
    tt("dve", a_t, lamre, dtt, ALU.mult, r=[("lamre",), ("dtt", 0), ("dtt", 1)], w=[("a_t",)])
    tt("dve", th_t, lamim, dtt, ALU.mult, r=[("lamim",), ("dtt", 0), ("dtt", 1)], w=[("th_t",)])
    kvb = kv.unsqueeze(1).to_broadcast([128, 32, 24])
    tt("dve", mag, a_t.unsqueeze(2).to_broadcast([128, 32, 24]), kvb, ALU.mult, r=[("a_t",), ("kv",)], w=[("mag",)])
    act(mag, mag, AF.Exp, r=[("mag",)], w=[("mag",)])
    tt("dve", ang, th_t.unsqueeze(2).to_broadcast([128, 32, 24]), kvb, ALU.mult, r=[("th_t",), ("kv",)], w=[("ang",)])

    def range_reduce(dst, src, shift, keys_r, key_w):
        ts("dve", angf, src, 1.0 / TWO_PI, shift / TWO_PI, ALU.mult, ALU.add, r=keys_r, w=[("angf",)])
        cp("dve", angi, angf, r=[("angf",)], w=[("angi",)])
        cp("dve", angf, angi, r=[("angi",)], w=[("angf",)])
        stt(dst, angf, -TWO_PI, src, ALU.mult, ALU.add, r=[("angf",)] + keys_r, w=[key_w])
        ts("dve", dst, dst, shift, math.pi, ALU.add, ALU.min, r=[key_w], w=[key_w])
        ts("dve", dst, dst, -math.pi, None, ALU.max, None, r=[key_w], w=[key_w])

    range_reduce(sinv, ang, 0.0, [("ang",)], ("sinv",))
    cp("dve", phiw, sinv[:, :, 15], r=[("sinv",)], w=[("phiw",)])
    act(sinv, sinv, AF.Sin, r=[("sinv",)], w=[("sinv",)])
    range_reduce(cosv, ang, math.pi / 2, [("ang",)], ("cosv",))
    act(cosv, cosv, AF.Sin, r=[("cosv",)], w=[("cosv",)])
    tt("dve", LPre, mag, cosv, ALU.mult, r=[("mag",), ("cosv",)], w=[("LPre",)])
    tt("dve", LPim, mag, sinv, ALU.mult, r=[("mag",), ("sinv",)], w=[("LPim",)])
    cp("dve", rho, mag[:, :, 15], r=[("mag",)], w=[("rho",)])
    cp("dve", fcos, cosv[:, :, 15], r=[("cosv",)], w=[("fcos",)])
    cp("dve", fsin, sinv[:, :, 15], r=[("sinv",)], w=[("fsin",)])

    dump("dtt", dtt, [("dtt", 0), ("dtt", 1)])
    dump("lamre", lamre, [("lamre",)])
    dump("a_t", a_t, [("a_t",)])
    dump("mag", mag, [("mag",)])
    dump("ang", ang, [("ang",)])
    dump("cosv", cosv, [("cosv",)])
    dump("sinv", sinv, [("sinv",)])
    dump("kv", kv, [("kv",)])
    t1 = M.alloc("t1", [128, 32], F32)
    t2 = M.alloc("t2", [128, 32], F32)
    t3 = M.alloc("t3", [128, 32], F32)
    cfr = M.alloc("cfr", [128, 32], F32)
    cfi = M.alloc("cfi", [128, 32], F32)
    kS = [("t1",), ("t2",), ("t3",)]
    ts("dve", t1, LPre[:, :, 8], -1.0, None, ALU.add, None, r=[("LPre",)], w=[("t1",)])
    tt("dve", t2, lamre, lamre, ALU.mult, r=[("lamre",)], w=[("t2",)])
    tt("dve", t3, lamim, lamim, ALU.mult, r=[("lamim",)], w=[("t3",)])
    tt("dve", t2, t2, t3, ALU.add, r=[("t2",), ("t3",)], w=[("t2",)])
    S.op("dve", lambda e: e.reciprocal(out=t2, in_=t2), r=[("t2",)], w=[("t2",)])
    tt("dve", cfr, t1, lamre, ALU.mult, r=[("t1",), ("lamre",)], w=[("cfr",)])
    tt("dve", t3, LPim[:, :, 8], lamim, ALU.mult, r=[("LPim",), ("lamim",)], w=[("t3",)])
    tt("dve", cfr, cfr, t3, ALU.add, r=[("cfr",), ("t3",)], w=[("cfr",)])
    tt("dve", cfr, cfr, t2, ALU.mult, r=[("cfr",), ("t2",)], w=[("cfr",)])
    tt("dve", cfi, LPim[:, :, 8], lamre, ALU.mult, r=[("LPim",), ("lamre",)], w=[("cfi",)])
    tt("dve", t3, t1, lamim, ALU.mult, r=[("t1",), ("lamim",)], w=[("t3",)])
    tt("dve", cfi, cfi, t3, ALU.subtract, r=[("cfi",), ("t3",)], w=[("cfi",)])
    tt("dve", cfi, cfi, t2, ALU.mult, r=[("cfi",), ("t2",)], w=[("cfi",)])

    bbr = M.alloc("bbr", [128, 32, 16], F32)
    bbi = M.alloc("bbi", [128, 32, 16], F32)
    tb = M.alloc("tb", [128, 32, 16], F32)
    cfrb = cfr.unsqueeze(2).to_broadcast([128, 32, 16])
    cfib = cfi.unsqueeze(2).to_broadcast([128, 32, 16])
    tt("dve", bbr, bre, cfrb, ALU.mult, r=[("bre",), ("cfr",)], w=[("bbr",)])
    tt("dve", tb, bim, cfib, ALU.mult, r=[("bim",), ("cfi",)], w=[("tb",)])
    tt("dve", bbr, bbr, tb, ALU.subtract, r=[("bbr",), ("tb",)], w=[("bbr",)])
    tt("dve", bbi, bim, cfrb, ALU.mult, r=[("bim",), ("cfr",)], w=[("bbi",)])
    tt("dve", tb, bre, cfib, ALU.mult, r=[("bre",), ("cfi",)], w=[("tb",)])
    tt("dve", bbi, bbi, tb, ALU.add, r=[("bbi",), ("tb",)], w=[("bbi",)])

    M.free("a_t", "th_t", "mag", "ang", "angi", "angf", "cosv", "sinv", "t1", "t2", "t3")
    Pre = M.alloc("Pre", [128, 32, 128], F32)
    Pim = M.alloc("Pim", [128, 32, 128], F32)
    CRr = M.alloc("CRr", [128, 32, 128], F32)
    CRi = M.alloc("CRi", [128, 32, 128], F32)
    tmpA = M.alloc("tmpA", [128, 32, 128], F32)
    sh4 = [128, 32, 8, 16]

    def v4(t):
        return t.rearrange("p a (b c) -> p a b c", b=8)

    def lpb(t, i0):
        return t[:, :, i0:i0 + 8].unsqueeze(3).to_broadcast(sh4)

    def chb(t):
        return t.unsqueeze(2).to_broadcast(sh4)

    def cmul(dst_re, dst_im, ar, ai, br_, bi_, keys_a, keys_b, kre, kim, neg_im=False, dre4=None, dim4=None):
        dre4 = v4(dst_re) if dre4 is None else dre4
        dim4 = v4(dst_im) if dim4 is None else dim4
        tA = v4(tmpA)
        tt("dve", tA, ar, br_, ALU.mult, r=keys_a + keys_b, w=[("tmpA",)])
        tt("dve", dim4, ai, bi_, ALU.mult, r=keys_a + keys_b, w=[kim])
        tt("dve", dre4, tA, dim4, ALU.subtract, r=[("tmpA",), kim], w=[kre])
        tt("dve", tA, ar, bi_, ALU.mult, r=keys_a + keys_b, w=[("tmpA",)])
        tt("dve", dim4, ai, br_, ALU.mult, r=keys_a + keys_b + [kre], w=[kim])
        if neg_im:
            stt(dim4, tA, -1.0, dim4, ALU.mult, ALU.subtract, r=[("tmpA",), kim], w=[kim])
        else:
            tt("dve", dim4, tA, dim4, ALU.add, r=[("tmpA",), kim], w=[kim])

    kLP = [("LPre",), ("LPim",)]
    cmul(Pre, Pim, lpb(LPre, 0), lpb(LPim, 0), chb(bbr), chb(bbi), kLP, [("bbr",), ("bbi",)], ("Pre",), ("Pim",))
    cmul(CRr, CRi, lpb(LPre, 8), lpb(LPim, 8), chb(cre), chb(cim), kLP, [("cre",), ("cim",)], ("CRr",), ("CRi",),
         neg_im=True)
    cp("dve", OW[:, :, 0, :], CRr, r=[("CRr",)], w=[("OW",)])
    cp("dve", OW[:, :, 1, :], CRi, r=[("CRi",)], w=[("OW",)])
    cmul(CRr, CRi, lpb(LPre, 16), lpb(LPim, 16), chb(cre), chb(cim), kLP, [("cre",), ("cim",)], ("CRr",), ("CRi",),
         neg_im=True)

    late_loads()
    for p in range(32):
        ps, pk = psum()
        tr(ps[:, 0:128], Pre[:, p, :], identf, r=[("Pre",), ("identf",)], w=[pk])
        tr(ps[:, 128:256], Pim[:, p, :], identf, r=[("Pim",), ("identf",)], w=[pk])
        cp(evac_eng(), ZW[:, p, :, :].rearrange("p a b -> p (a b)"), ps[:, 0:256], r=[pk], w=[("ZW",)])
    kwtmp = M.alloc("kwtmp", [128, 4, 128], F32)
    for p in range(32):
        for g2 in range(2):
            g = 2 * p + g2
            ps, pk = psum()
            sl = slice(64 * g2, 64 * g2 + 64)
            mm(ps[:, 0:128], Pre[sl, p, :], CRr[sl, p, :], True, False, r=[("Pre",), ("CRr",)], w=[pk])
            mm(ps[:, 0:128], Pim[sl, p, :], CRi[sl, p, :], False, True, r=[("Pim",), ("CRi",)], w=[pk])
            kt = kwtmp[:, g % 4, :]
            tt("dve", kt, ps[:, 0:128], maskt, ALU.mult, r=[pk, ("maskt",)], w=[("kwtmp", g % 4)])
            stt(KW[:, g, :], identf, dsk[:, g:g + 1], kt, ALU.mult, ALU.add,
                r=[("identf",), ("kwtmp", g % 4)] + [("dsk", i) for i in range(8)], w=[("KW", g)])
    M.free("LPre", "LPim", "cfr", "cfi",
           "bbr", "bbi", "tb", "Pre", "Pim", "CRr", "CRi", "tmpA", "kwtmp",
           "lamre", "lamim", "dtt", "bre", "bim", "cre", "cim", "kv", "maskt")

    NW = 3
    wslots = []
    w_rr = [0]

    def alloc_wslots(n=NW):
        wslots.clear()
        wslots.extend(M.alloc(f"wslot{i}", [128, 16, 256], BF16) for i in range(n))

    def free_wslots():
        M.free(*[f"wslot{i}" for i in range(len(wslots))])

    alloc_wslots(2)

    def wtile(W, k0, KC, c0):
        i = w_rr[0] % len(wslots)
        w_rr[0] += 1
        src = W[k0:k0 + 128 * KC, c0:c0 + 256].rearrange("(kc p) m -> p kc m", p=128)
        dma("pool", wslots[i][:, 0:KC, :], src, r=(), w=[(f"wslot{i}",)], group="w", slot=i)
        return wslots[i], (f"wslot{i}",)

    def proj(wt, wk, KC, mo, rhs_list, r_keys):
        outs = []
        for (rf, n) in rhs_list:
            ps, pk = psum()
            outs.append((ps, pk, n))
        for kc in range(KC):
            for ri, ((rf, n), (ps, pk, _)) in enumerate(zip(rhs_list, outs)):
                rk = r_keys(kc, ri) if callable(r_keys) else r_keys
                mm(ps[:, 0:n], wt[:, kc, mo:mo + 128], rf(kc), kc == 0, kc == KC - 1, r=[wk] + rk, w=[pk])
        return outs

    xin, xnb, junkl = [], [], []
    ssq = M.alloc("ssq", [128, 4], F32)
    ld_rr = [0]

    def alloc_load_bufs():
        xin.clear(); xnb.clear()
        xin.extend(M.alloc(f"xin{i}", [128, D], F32) for i in range(3))
        xnb.extend(M.alloc(f"xnb{i}", [128, D], BF16) for i in range(2))

    def free_load_bufs():
        M.free("xin0", "xin1", "xin2", "xnb0", "xnb1")

    alloc_load_bufs()

    def load_tokens(src, ntok, hb, hbname, xres=None):
        tiles = []
        t0 = 0
        while t0 < ntok:
            nt = min(128, ntok - t0)
            tiles.append((t0, nt, ld_rr[0]))
            ld_rr[0] += 1
            t0 += nt

        def stage_a(t0, nt, k):
            i, j = k % 3, k % 2
            kx, kn, ks = (f"xin{i}",), (f"xnb{j}",), ("ssq", j)
            dma("sp", xin[i][0:nt, :], src[t0:t0 + nt, :], r=(), w=[kx], group="x", slot=i)
            if hb is not None:
                act(xnb[j][0:nt, :], xin[i][0:nt, :], AF.Square, r=[kx], w=[kn, ks], accum=ssq[0:nt, j:j + 1])
                act(ssq[0:nt, j:j + 1], ssq[0:nt, j:j + 1], AF.Sqrt, r=[ks], w=[ks], scale=1.0 / D, bias=EPS)
                S.op("dve", lambda e, j=j, nt=nt: e.reciprocal(out=ssq[0:nt, j:j + 1], in_=ssq[0:nt, j:j + 1]), r=[ks], w=[ks])
                stt(xnb[j][0:nt, :], xin[i][0:nt, :], ssq[0:nt, j:j + 1], gmix_bc[0:nt, :], ALU.mult, ALU.mult,
                    r=[kx, ks, ("gmix_bc",)], w=[kn])

        def stage_b(t0, nt, k):
            i, j = k % 3, k % 2
            kx, kn = (f"xin{i}",), (f"xnb{j}",)
            for q in range(4):
                if hb is not None:
                    ps, pk = psum()
                    psb = ps.bitcast(BF16)
                    for kk in range(4):
                        kc = 4 * q + kk
                        tr(psb[:, kk * 128:kk * 128 + nt], xnb[j][0:nt, kc * 128:(kc + 1) * 128], identb[0:nt, 0:nt],
                           r=[kn, ("identb",)], w=[pk])
                    cp(evac_eng(), hb[:, 4 * q:4 * q + 4, t0:t0 + nt],
                       psb[:, 0:512].rearrange("p (a b) -> p a b", a=4)[:, :, 0:nt], r=[pk], w=[(hbname, t0)])
                if xres is not None:
                    ps2, pk2 = psum()
                    for kk in range(4):
                        kc = 4 * q + kk
                        tr(ps2[:, kk * 128:kk * 128 + nt], xin[i][0:nt, kc * 128:(kc + 1) * 128], identf[0:nt, 0:nt],
                           r=[kx, ("identf",)], w=[pk2])
                    cp(evac_eng(), xres[:, 4 * q:4 * q + 4, t0:t0 + nt],
                       ps2[:, 0:512].rearrange("p (a b) -> p a b", a=4)[:, :, 0:nt], r=[pk2], w=[("xres", t0 // 512)])

        stage_a(*tiles[0])
        for k in range(len(tiles)):
            if k + 1 < len(tiles):
                stage_a(*tiles[k + 1])
            stage_b(*tiles[k])

    def hb_keys(name, ntok):
        return [(name, t) for t in range(0, ntok, 128)]

    u_fm = M.alloc("u_fm", [128, 8, NP + NM], BF16)
    hb_pre = M.alloc("hb_pre", [128, 16, NP], BF16)
    hb_halo = M.alloc("hb_halo", [128, 16, 30], BF16)
    load_tokens(xp_d, NP, hb_pre, "hb_pre")
    kpre = hb_keys("hb_pre", NP)
    cp("dve", hb_halo, hb_pre[:, :, NP - 30:NP], r=kpre, w=[("hb_halo",)])
    for mm_ in range(4):
        wt, wk = wtile(w_in, 0, 16, 256 * mm_)
        for sub in range(2):
            mc = 2 * mm_ + sub
            rl = [(lambda kc, a=a, n=n: hb_pre[:, kc, a:a + n], n) for (a, n) in ((0, 512), (512, 512), (1024, 16))]
            outs = proj(wt, wk, 16, 128 * sub, rl, kpre)
            for (a, n), (ps, pk, _) in zip(((0, 512), (512, 512), (1024, 16)), outs):
                cp(evac_eng(), u_fm[:, mc, 0:NP].rearrange("p (i c) -> p c i", i=8)[:, a // 8:(a + n) // 8, :],
                   ps[:, 0:n].rearrange("p (c i) -> p c i", i=8), r=[pk], w=[("u_fm", mc, "pre")])
    M.free("hb_pre")

    hb = M.alloc("hb", [128, 16, NM], BF16)
    load_tokens(xm_d, NM, hb, "hb")
    M.free("gmix_bc")
    khb = hb_keys("hb", NM)
    dump("hb", hb, khb)
    main_rl = [(lambda kc, a=a: hb[:, kc, a:a + 512], 512) for a in (0, 512)]
    for mm_ in range(4):
        wt, wk = wtile(w_in, 0, 16, 256 * mm_)
        for sub in range(2):
            mc = 2 * mm_ + sub
            outs = proj(wt, wk, 16, 128 * sub, main_rl, khb)
            for a, (ps, pk, _) in zip((0, 512), outs):
                cp(evac_eng(), u_fm[:, mc, NP:NP + NM].rearrange("p (i c) -> p c i", i=8)[:, a // 8:(a + 512) // 8, :],
                   ps[:, 0:512].rearrange("p (c i) -> p c i", i=8), r=[pk], w=[("u_fm", mc, "main")])

    free_load_bufs()
    free_wslots()
    dump("u_fm", u_fm, [("u_fm", mc, w_) for mc in range(8) for w_ in ("pre", "main")])
    dump("ZW", ZW, [("ZW",)])
    dump("KW", KW, [("KW", g) for g in range(64)])
    dump("OW", OW, [("OW",)])
    dump("rho", rho, [("rho",)])
    dump("fcos", fcos, [("fcos",)])
    dump("fsin", fsin, [("fsin",)])

    NCT = NCP + NCM
    u_scr = M.alloc("u_scr", [128, 64, NCT], BF16)
    for g in range(64):
        b, gp = divmod(g, 8)
        ps, pk = psum()
        upre = u_fm[:, b, 0:NP].rearrange("p (i c) -> p i c", i=8)
        umain = u_fm[:, b, NP:NP + NM].rearrange("p (i c) -> p i c", i=8)
        for i in range(8):
            mm(ps[:, 0:NCP], sel[:, gp * 8 + i, :], upre[:, i, :], i == 0, i == 7,
               r=[("sel",), ("u_fm", b, "pre")], w=[pk])
        for i in range(8):
            mm(ps[:, NCP:NCT], sel[:, gp * 8 + i, :], umain[:, i, :], i == 0, i == 7,
               r=[("sel",), ("u_fm", b, "main")], w=[pk])
        cp(evac_eng(), u_scr[:, g, :], ps[:, 0:NCT], r=[pk], w=[("u_scr", g)])
    M.free("u_fm")

    PQ = 4
    Sst = M.alloc("Sst", [128, PQ, 2, NCM], BF16)
    z_scr = M.alloc("z_scr", [128, 2 * PQ, NCM], BF16)
    z_fm = M.alloc("z_fm", [128, 8, NM], BF16)
    NQ = NCP
    Ecs = [M.alloc(f"Ec{i}", [128, PQ, NQ], F32) for i in range(2)]
    Ess = [M.alloc(f"Es{i}", [128, PQ, NQ], F32) for i in range(2)]
    eang = M.alloc("eang", [128, PQ, NQ], F32)
    ekf = M.alloc("ekf", [128, PQ, NQ], F32)
    eki = M.alloc("eki", [128, PQ, NQ], I32)
    Zss = [M.alloc(f"Zs{i}", [128, PQ, 2, NQ], F32) for i in range(2)]
    bufA = M.alloc("bufA", [128, PQ * NQ], F32)
    bufB = M.alloc("bufB", [128, PQ * NQ], F32)
    bufC = M.alloc("bufC", [128, PQ * NQ], F32)
    rtab = M.alloc("rtab", [128, PQ * NQ], F32)
    vinit = M.alloc("vinit", [128, PQ, 2], F32)
    vtmp = M.alloc("vtmp", [128, PQ, 2], F32)

    def v3(buf, n):
        return buf[:, 0:PQ * n].rearrange("p (a b) -> p a b", a=PQ)

    def emit_Z(bt):
        p0 = PQ * bt
        for stage in range(2):
            nz = NCP if stage == 0 else NCM
            c0 = 0 if stage == 0 else NCP
            for pl in range(PQ):
                p = p0 + pl
                for ri in range(2):
                    ps, pk = psum()
                    for g2 in range(2):
                        g = 2 * p + g2
                        mm(ps[64 * g2:64 * g2 + 64, 0:nz], ZW[:, p, ri, 64 * g2:64 * g2 + 64], u_scr[:, g, c0:c0 + nz],
                           True, True, r=[("ZW",), ("u_scr", g)], w=[pk])
                    cp("act", Zss[stage][:, pl, ri, 0:nz], ps[:, 0:nz], r=[pk], w=[(f"Zs{stage}", pl, ri)])

    def emit_E(bt):
        p0 = PQ * bt
        Ec, Es = Ecs[bt % 2], Ess[bt % 2]
        kE = [(f"Ec{bt % 2}",), (f"Es{bt % 2}",)]
        kEc, kEs = kE[0], kE[1]
        tt("dve", eang, phiw[:, p0:p0 + PQ].unsqueeze(2).to_broadcast([128, PQ, NQ]),
           qv.unsqueeze(1).to_broadcast([128, PQ, NQ]), ALU.mult, r=[("phiw",), ("qv",)], w=[("eang",)])
        ts("dve", ekf, eang, 1.0 / TWO_PI, None, ALU.mult, None, r=[("eang",)], w=[("ekf",)])
        cp("dve", eki, ekf, r=[("ekf",)], w=[("eki",)])
        cp("dve", ekf, eki, r=[("eki",)], w=[("ekf",)])
        stt(Es, ekf, -TWO_PI, eang, ALU.mult, ALU.add, r=[("ekf",), ("eang",)], w=[kEs])
        ts("dve", Es, Es, math.pi, -math.pi, ALU.min, ALU.max, r=[kEs], w=[kEs])
        stt(Ec, Es, -1.0, Es, ALU.mult, ALU.max, r=[kEs], w=[kEc])
        act(Ec, Ec, AF.Sin, r=[kEc], w=[kEc], scale=-1.0, bias=math.pi / 2)
        act(Es, Es, AF.Sin, r=[kEs], w=[kEs])

    for bt in range(32 // PQ):
        p0 = PQ * bt
        Ec, Es = Ecs[bt % 2], Ess[bt % 2]
        kE = [(f"Ec{bt % 2}",), (f"Es{bt % 2}",)]
        if bt == 0:
            emit_E(0)
        if bt == 0:
            emit_Z(0)

        for stage in range(2):
            n = NCP if stage == 0 else NCM + 1
            nz = NCP if stage == 0 else NCM
            c0 = 0 if stage == 0 else NCP
            zoff = 0 if stage == 0 else 1
            Zs = Zss[stage]
            kZ = [(f"Zs{stage}", pl, ri) for pl in range(PQ) for ri in range(2)]
            A3, B3, C3, R3 = v3(bufA, n), v3(bufB, n), v3(bufC, n), v3(rtab, n)
            Ecz, Esz = Ec[:, :, 0:nz], Es[:, :, 0:nz]
            Zr, Zi = Zs[:, :, 0, 0:nz], Zs[:, :, 1, 0:nz]
            sl = slice(zoff, zoff + nz)
            cp("dve", R3, rho[:, p0:p0 + PQ].unsqueeze(2).to_broadcast([128, PQ, n]), r=[("rho",)], w=[("rtab",)])
            tt("dve", A3[:, :, sl], Ecz, Zr, ALU.mult, r=kE + kZ, w=[("bufA",)])
            tt("dve", B3[:, :, sl], Esz, Zi, ALU.mult, r=kE + kZ, w=[("bufB",)])
            memset("dve", R3[:, :, 0:1], 0.0, w=[("rtab",)])
            tt("dve", C3[:, :, sl], Ecz, Zi, ALU.mult, r=kE + kZ, w=[("bufC",)])
            tt("dve", Zr, Esz, Zr, ALU.mult, r=kE + kZ, w=kZ)
            tt("dve", A3[:, :, sl], A3[:, :, sl], B3[:, :, sl], ALU.add, r=[("bufA",), ("bufB",)], w=[("bufA",)])
            tt("dve", C3[:, :, sl], C3[:, :, sl], Zr, ALU.subtract, r=[("bufC",)] + kZ, w=[("bufC",)])
            if stage == 1:
                cp("dve", A3[:, :, 0], vinit[:, :, 0], r=[("vinit", 0), ("bufA",)], w=[("bufA",)])
                cp("dve", C3[:, :, 0], vinit[:, :, 1], r=[("vinit", 1), ("bufC",)], w=[("bufC",)])
            fl = slice(0, PQ * n)
            S.op("dve", lambda e, fl=fl: e.tensor_tensor_scan(out=bufA[:, fl], data0=rtab[:, fl], data1=bufA[:, fl],
                                                            initial=0.0, op0=ALU.mult, op1=ALU.add),
                 r=[("rtab",), ("bufA",)], w=[("bufA",)])
            S.op("dve", lambda e, fl=fl: e.tensor_tensor_scan(out=bufC[:, fl], data0=rtab[:, fl], data1=bufC[:, fl],
                                                            initial=0.0, op0=ALU.mult, op1=ALU.add),
                 r=[("rtab",), ("bufC",)], w=[("bufC",)])
            if stage == 0:
                ec, es_ = Ec[:, :, NCP - 1], Es[:, :, NCP - 1]
                vr, vi = A3[:, :, NCP - 1], C3[:, :, NCP - 1]
                tt("dve", vinit[:, :, 0], ec, vr, ALU.mult, r=kE + [("bufA",)], w=[("vinit", 0)])
                tt("dve", vtmp[:, :, 0], es_, vi, ALU.mult, r=kE + [("bufC",)], w=[("vtmp", 0)])
                tt("dve", vinit[:, :, 1], ec, vi, ALU.mult, r=kE + [("bufC",)], w=[("vinit", 1)])
                tt("dve", vtmp[:, :, 1], es_, vr, ALU.mult, r=kE + [("bufA",)], w=[("vtmp", 1)])
                tt("dve", vinit[:, :, 0], vinit[:, :, 0], vtmp[:, :, 0], ALU.subtract, r=[("vinit", 0), ("vtmp", 0)], w=[("vinit", 0)])
                tt("dve", vinit[:, :, 1], vinit[:, :, 1], vtmp[:, :, 1], ALU.add, r=[("vinit", 1), ("vtmp", 1)], w=[("vinit", 1)])
            else:
                kS_ = [("Sst",)]
                nm1 = NCM - 1
                Vr, Vi = A3[:, :, 1:NCM], C3[:, :, 1:NCM]
                Ecm, Esm = Ec[:, :, 0:nm1], Es[:, :, 0:nm1]
                Bv = v3(bufB, nm1)
                Rv = v3(rtab, nm1)
                B2 = eang[:, :, 0:nm1]
                R2 = ekf[:, :, 0:nm1]
                tt("dve", Bv, Ecm, Vr, ALU.mult, r=kE + [("bufA",)], w=[("bufB",)])
                tt("dve", Rv, Esm, Vi, ALU.mult, r=kE + [("bufC",)], w=[("rtab",)])
                tt("dve", B2, Ecm, Vi, ALU.mult, r=kE + [("bufC",)], w=[("eang",)])
                tt("dve", R2, Esm, Vr, ALU.mult, r=kE + [("bufA",)], w=[("ekf",)])
                cp("dve", Sst[:, :, 0, 0], A3[:, :, 0], r=[("bufA",)], w=[("Sst", 0, 0)])
                cp("dve", Sst[:, :, 1, 0], C3[:, :, 0], r=[("bufC",)], w=[("Sst", 1, 0)])
                tt("dve", Sst[:, :, 0, 1:NCM], Bv, Rv, ALU.subtract, r=[("bufB",), ("rtab",)], w=[("Sst", 0, 1)])
                tt("dve", Sst[:, :, 1, 1:NCM], B2, R2, ALU.add, r=[("eang",), ("ekf",)], w=[("Sst", 1, 1)])

        if bt + 1 < 32 // PQ:
            emit_E(bt + 1)
            emit_Z(bt + 1)
        for pl in range(PQ):
            p = p0 + pl
            for g2 in range(2):
                g = 2 * p + g2
                gl = 2 * pl + g2
                if gl % 4 == 0:
                    ps, pk = psum()
                o = ps[:, 128 * (gl % 4):128 * (gl % 4) + 128]
                sl = slice(64 * g2, 64 * g2 + 64)
                mm(o, KW[:, g, :], u_scr[:, g, NCP:NCT], True, False, r=[("KW", g), ("u_scr", g)], w=[pk])
                mm(o, OW[sl, p, 0, :], Sst[sl, pl, 0, :], False, False, r=[("OW",)] + [("Sst", a_, b_) for a_ in range(2) for b_ in range(2)], w=[pk])
                mm(o, OW[sl, p, 1, :], Sst[sl, pl, 1, :], False, True, r=[("OW",)] + [("Sst", a_, b_) for a_ in range(2) for b_ in range(2)], w=[pk])
                if gl % 4 == 3:
                    act(z_scr[:, gl - 3:gl + 1, :].rearrange("p a b -> p (a b)"), ps[:, 0:512], AF.Gelu_apprx_tanh,
                        r=[pk], w=[("z_scr", gl // 4)])
        for bl in range(1):
            b = bt
            zv = z_fm[:, b, :].rearrange("p (m j) -> p j m", j=8)
            for jh in range(2):
                ps, pk = psum()
                for jj in range(4):
                    j = 4 * jh + jj
                    for gp in range(8):
                        mm(ps[:, 128 * jj:128 * jj + 128], sel[:, j * 8 + gp, :], z_scr[:, 8 * bl + gp, :], gp == 0, gp == 7,
                           r=[("sel",)] + [("z_scr", k) for k in range(2)], w=[pk])
                cp("act", zv[:, 4 * jh:4 * jh + 4, :], ps[:, 0:512].rearrange("p (a b) -> p a b", a=4),
                   r=[pk], w=[("z_fm", b)])
    dump("z_fm", z_fm, [("z_fm", b) for b in range(8)])
    dump("u_scr", u_scr, [("u_scr", g) for g in range(64)])
    M.free("u_scr", "Sst", "z_scr", "Ec0", "Ec1", "Es0", "Es1", "eang", "ekf", "eki", "Zs0", "Zs1", "bufA", "bufB", "bufC", "rtab", "vinit", "vtmp")
    M.free("ZW", "KW", "OW", "rho", "fcos", "fsin", "dsk", "sel", "phiw", "qv")

    alloc_wslots(6)
    cbuf = M.alloc("cbuf", [128, 8, 30 + NM], BF16)
    sigT = M.alloc("sigT", [128, 2, 30 + NM], BF16)
    cv_rl = main_rl + [(lambda kc: hb_halo[:, kc, :], 30)]
    cv_cols = ((30, 512), (30 + 512, 512), (0, 30))
    for jj in range(4):
        wt2, wk2 = wtile(w_in, 0, 16, 2048 + 256 * jj)
        wt1, wk1 = wtile(w_in, 0, 16, 1024 + 256 * jj)
        for sub in range(2):
            outs = proj(wt2, wk2, 16, 128 * sub, cv_rl, khb + [("hb_halo",)])
            for (a, n), (ps, pk, _) in zip(cv_cols, outs):
                act(sigT[:, sub, a:a + n], ps[:, 0:n], AF.Sigmoid, r=[pk], w=[("sigT", sub, a)])
        for sub in range(2):
            j = 2 * jj + sub
            outs = proj(wt1, wk1, 16, 128 * sub, cv_rl, khb + [("hb_halo",)])
            for (a, n), (ps, pk, _) in zip(cv_cols, outs):
                tt("dve", cbuf[:, j, a:a + n], ps[:, 0:n], sigT[:, sub, a:a + n], ALU.mult,
                   r=[pk, ("sigT", sub, a)], w=[("cbuf", j)])
    M.free("sigT", "hb_halo")
    dump("cbuf", cbuf, [("cbuf", j) for j in range(8)])

    merged = M.alloc("merged", [128, 16, NM], BF16)
    g1 = [M.alloc(f"g1_{i}", [128, 512], F32) for i in range(2)]
    g2b = [M.alloc(f"g2_{i}", [128, 512], F32) for i in range(2)]
    g_rr = [0]
    kz = [("z_fm", b) for b in range(8)]
    z_rl = [(lambda kc, a=a: z_fm[:, kc, a:a + 512], 512) for a in (0, 512)]
    for mm_ in range(8):
        wtA, wkA = wtile(w_glu, 0, 8, 256 * mm_)
        wtB, wkB = wtile(w_glu, 0, 8, 2048 + 256 * mm_)
        wtG, wkG = wtile(w_in, 0, 16, 3072 + 256 * mm_)
        for sub in range(2):
            mc = 2 * mm_ + sub
            oB = proj(wtB, wkB, 8, 128 * sub, z_rl, kz)
            oG = proj(wtG, wkG, 16, 128 * sub, main_rl, khb)
            oA = proj(wtA, wkA, 8, 128 * sub, z_rl, kz)
            for nt in range(2):
                i = g_rr[0] % 2
                g_rr[0] += 1
                a = 512 * nt
                act(g1[i], oB[nt][0][:, 0:512], AF.Sigmoid, r=[oB[nt][1]], w=[(f"g1_{i}",)])
                act(g2b[i], oG[nt][0][:, 0:512], AF.Sigmoid, r=[oG[nt][1]], w=[(f"g2_{i}",)])
                tt("dve", g1[i], g1[i], g2b[i], ALU.mult, r=[(f"g1_{i}",), (f"g2_{i}",)], w=[(f"g1_{i}",)])
                tt("dve", merged[:, mc, a:a + 512], oA[nt][0][:, 0:512], g1[i], ALU.mult,
                   r=[oA[nt][1], (f"g1_{i}",)], w=[("merged", mc, nt)])
    M.free("z_fm")
    free_wslots()
    alloc_wslots(4)
    dump("merged_a", merged, [("merged", mc, nt) for mc in range(16) for nt in range(2)])

    convout = M.alloc("convout", [128, 8, NM], F32)
    diag = [M.alloc(f"diag{i}", [128, 31, 128], BF16) for i in range(2)]
    sqb = [M.alloc(f"sqb{i}", [128, NM], BF16) for i in range(2)]
    cbb = [M.alloc(f"cbb{i}", [128, NM], BF16) for i in range(2)]
    conv_banks, stat_banks = [0, 1, 2, 3], [4, 5, 6, 7]
    cb_ctr = [0]
    st_ps = [psum(bk) for bk in stat_banks]
    for j in range(8):
        di = j % 2
        for k in range(31):
            ts("dve", diag[di][:, k, :], identf, convw[:, j, k:k + 1], None, ALU.mult, None,
               r=[("identf",), ("convw",)], w=[(f"diag{di}", k)])
        for nt in range(2):
            ps, pk = psum_from(conv_banks, cb_ctr)
            for k in range(31):
                mm(ps[:, 0:512], diag[di][:, k, :], cbuf[:, j, 512 * nt + k:512 * nt + k + 512], k == 0, k == 30,
                   r=[(f"diag{di}", k), ("cbuf", j)], w=[pk])
            act(convout[:, j, 512 * nt:512 * nt + 512], ps[:, 0:512], AF.Identity, r=[pk], w=[("convout", j, nt)],
                bias=convb[:, j:j + 1])
        kco = [("convout", j, 0), ("convout", j, 1)]
        act(sqb[di], convout[:, j, :], AF.Square, r=kco, w=[(f"sqb{di}",)])
        cp("dve", cbb[di], convout[:, j, :], r=kco, w=[(f"cbb{di}",)])
        for nt in range(2):
            mm(st_ps[nt][0][:, 0:512], onesb, cbb[di][:, 512 * nt:512 * nt + 512], j == 0, j == 7,
               r=[("onesb",), (f"cbb{di}",)], w=[st_ps[nt][1]])
            mm(st_ps[2 + nt][0][:, 0:512], onesb, sqb[di][:, 512 * nt:512 * nt + 512], j == 0, j == 7,
               r=[("onesb",), (f"sqb{di}",)], w=[st_ps[2 + nt][1]])
    M.free("diag0", "diag1", "sqb0", "sqb1", "cbb0", "cbb1", "cbuf")
    mean_t = M.alloc("mean_t", [128, NM], F32)
    rstd_t = M.alloc("rstd_t", [128, NM], F32)
    msq_t = M.alloc("msq_t", [128, 512], F32)
    for nt in range(2):
        sl = slice(512 * nt, 512 * nt + 512)
        ts("dve", mean_t[:, sl], st_ps[nt][0][:, 0:512], 1.0 / 1024, None, ALU.mult, None, r=[st_ps[nt][1]], w=[("mean_t", nt)])
        tt("dve", msq_t, mean_t[:, sl], mean_t[:, sl], ALU.mult, r=[("mean_t", nt)], w=[("msq_t",)])
        stt(rstd_t[:, sl], st_ps[2 + nt][0][:, 0:512], 1.0 / 1024, msq_t, ALU.mult, ALU.subtract,
            r=[st_ps[2 + nt][1], ("msq_t",)], w=[("rstd_t", nt)])
        ts("dve", rstd_t[:, sl], rstd_t[:, sl], 0.0, None, ALU.max, None, r=[("rstd_t", nt)], w=[("rstd_t", nt)])
        act(rstd_t[:, sl], rstd_t[:, sl], AF.Sqrt, r=[("rstd_t", nt)], w=[("rstd_t", nt)], scale=1.0, bias=EPS)
        S.op("dve", lambda e, sl=sl: e.reciprocal(out=rstd_t[:, sl], in_=rstd_t[:, sl]), r=[("rstd_t", nt)], w=[("rstd_t", nt)])
    csb = M.alloc("csb", [128, 8, NM], BF16)
    kst = [("mean_t", 0), ("mean_t", 1), ("rstd_t", 0), ("rstd_t", 1)]
    for j in range(8):
        kco = [("convout", j, 0), ("convout", j, 1)]
        tt("dve", convout[:, j, :], convout[:, j, :], mean_t, ALU.subtract, r=kco + kst, w=kco)
        tt("dve", convout[:, j, :], convout[:, j, :], rstd_t, ALU.mult, r=kco + kst, w=kco)
        act(csb[:, j, :], convout[:, j, :], AF.Silu, r=kco + [("lng",), ("lnb",)], w=[("csb", j)],
            scale=lng[:, j:j + 1], bias=lnb[:, j:j + 1])
    dump("csb", csb, [("csb", j) for j in range(8)])
    M.free("convout", "mean_t", "rstd_t", "msq_t")
    kcs = [("csb", j) for j in range(8)]
    cs_rl = [(lambda kc, a=a: csb[:, kc, a:a + 512], 512) for a in (0, 512)]
    for mm_ in range(8):
        wtW, wkW = wtile(w_co, 0, 8, 256 * mm_)
        wtG, wkG = wtile(w_in, 0, 16, 5120 + 256 * mm_)
        for sub in range(2):
            mc = 2 * mm_ + sub
            oG = proj(wtG, wkG, 16, 128 * sub, main_rl, khb)
            oW = proj(wtW, wkW, 8, 128 * sub, cs_rl, lambda kc, ri: [("csb", kc)])
            for nt in range(2):
                i = g_rr[0] % 2
                g_rr[0] += 1
                a = 512 * nt
                act(g2b[i], oG[nt][0][:, 0:512], AF.Sigmoid, r=[oG[nt][1]], w=[(f"g2_{i}",)])
                tt("dve", g1[i], oW[nt][0][:, 0:512], g2b[i], ALU.mult, r=[oW[nt][1], (f"g2_{i}",)], w=[(f"g1_{i}",)])
                tt("dve", merged[:, mc, a:a + 512], merged[:, mc, a:a + 512], g1[i], ALU.add,
                   r=[("merged", mc, nt), (f"g1_{i}",)], w=[("merged", mc, nt)])
    M.free("csb", "hb", "g1_0", "g1_1", "g2_0", "g2_1")

    dump("merged", merged, [("merged", mc, nt) for mc in range(16) for nt in range(2)])
    xres = M.alloc("xres", [128, 16, NM], F32)
    alloc_load_bufs()
    load_tokens(xm_d, NM, None, None, xres=xres)
    free_load_bufs()
    kmg = [("merged", mc, nt) for mc in range(16) for nt in range(2)]
    mg_rl = [(lambda kc, a=a: merged[:, kc, a:a + 512], 512) for a in (0, 512)]
    for mm_ in range(8):
        wt, wk = wtile(w_out, 0, 16, 256 * mm_)
        for sub in range(2):
            mc = 2 * mm_ + sub
            outs = proj(wt, wk, 16, 128 * sub, mg_rl, lambda kc, ri: [("merged", kc, ri)])
            for nt, (ps, pk, _) in enumerate(outs):
                sl = slice(512 * nt, 512 * nt + 512)
                tt("dve", xres[:, mc, sl], xres[:, mc, sl], ps[:, 0:512], ALU.add, r=[pk, ("xres", nt)], w=[("xres", nt)])
    M.free("merged")
    dump("xres1", xres, [("xres", 0), ("xres", 1)])

    hb2 = M.alloc("hb2", [128, 16, NM], BF16)
    sq2 = [M.alloc(f"sq2_{i}", [128, 512], BF16) for i in range(2)]
    rstd2 = M.alloc("rstd2", [128, NM], F32)
    for nt in range(2):
        sl = slice(512 * nt, 512 * nt + 512)
        ps, pk = psum()
        for kc in range(16):
            i = kc % 2
            act(sq2[i], xres[:, kc, sl], AF.Square, r=[("xres", nt)], w=[(f"sq2_{i}",)])
            mm(ps[:, 0:512], onesb, sq2[i], kc == 0, kc == 15, r=[("onesb",), (f"sq2_{i}",)], w=[pk])
        act(rstd2[:, sl], ps[:, 0:512], AF.Sqrt, r=[pk], w=[("rstd2", nt)], scale=1.0 / D, bias=EPS)
        S.op("dve", lambda e, sl=sl: e.reciprocal(out=rstd2[:, sl], in_=rstd2[:, sl]), r=[("rstd2", nt)], w=[("rstd2", nt)])
        for kc in range(16):
            stt(hb2[:, kc, sl], xres[:, kc, sl], gffn[:, kc:kc + 1], rstd2[:, sl], ALU.mult, ALU.mult,
                r=[("xres", nt), ("gffn",), ("rstd2", nt)], w=[("hb2", "f", nt)])
    kh2 = [("hb2", "f", 0), ("hb2", "f", 1)]
    h2_rl = [(lambda kc, a=a: hb2[:, kc, a:a + 512], 512) for a in (0, 512)]

    M.free("rstd2")
    hid = [M.alloc("hid0", [128, 16, NM], BF16)]
    rl_t = [M.alloc(f"rl_t{i}", [128, 512], F32) for i in range(2)]
    r_rr = [0]
    for hbk in range(4):
        hd = hid[0]
        hn = "hid0"
        for mm_ in range(8):
            wt, wk = wtile(w_ff1, 0, 16, 2048 * hbk + 256 * mm_)
            for sub in range(2):
                mc = 2 * mm_ + sub
                outs = proj(wt, wk, 16, 128 * sub, h2_rl, kh2)
                for nt, (ps, pk, _) in enumerate(outs):
                    i = r_rr[0] % 2
                    r_rr[0] += 1
                    act(rl_t[i], ps[:, 0:512], AF.Relu, r=[pk], w=[(f"rl_t{i}",)])
                    tt("dve", hd[:, mc, 512 * nt:512 * nt + 512], rl_t[i], rl_t[i], ALU.mult,
                       r=[(f"rl_t{i}",)], w=[(hn, mc, nt)])
        khid = [(hn, mc, nt) for mc in range(16) for nt in range(2)]
        hd_rl = [(lambda kc, a=a, hd=hd: hd[:, kc, a:a + 512], 512) for a in (0, 512)]
        for mm_ in range(8):
            wt, wk = wtile(w_ff2, 2048 * hbk, 16, 256 * mm_)
            for sub in range(2):
                mc = 2 * mm_ + sub
                outs = proj(wt, wk, 16, 128 * sub, hd_rl, lambda kc, ri, hn=hn: [(hn, kc, ri)])
                for nt, (ps, pk, _) in enumerate(outs):
                    sl = slice(512 * nt, 512 * nt + 512)
                    tt("dve", xres[:, mc, sl], xres[:, mc, sl], ps[:, 0:512], ALU.add, r=[pk, ("xres", nt)], w=[("xres", nt)])
    M.free("hid0", "rl_t0", "rl_t1", "hb2", "sq2_0", "sq2_1")

    dump("xres2", xres, [("xres", 0), ("xres", 1)])
    gf_bc = M.alloc("gf_bc", [128, D], F32)
    dma("sp", gf_bc, g_f_d.partition_broadcast(128), r=(), w=[("gf_bc",)])
    junkl.clear()
    junkl.append(M.alloc("junk", [128, D], BF16))
    ytm = [M.alloc(f"ytm{i}", [128, D], F32) for i in range(2)]
    yo = [M.alloc(f"yo{i}", [128, D], F32) for i in range(2)]
    fss = M.alloc("fss", [128, 2], F32)
    for tt_ in range(8):
        i = tt_ % 2
        nt = tt_ // 4
        for q in range(4):
            ps, pk = psum()
            for kk in range(4):
                kc = 4 * q + kk
                tr(ps[:, 128 * kk:128 * kk + 128], xres[:, kc, 128 * tt_:128 * tt_ + 128], identf,
                   r=[("xres", nt), ("identf",)], w=[pk])
            cp(evac_eng(), ytm[i][:, 512 * q:512 * q + 512], ps[:, 0:512], r=[pk], w=[(f"ytm{i}", q)])
        ky = [(f"ytm{i}", q) for q in range(4)]
        act(junkl[0][:, :], ytm[i], AF.Square, r=ky, w=[("junk",), ("fss", i)], accum=fss[:, i:i + 1])
        act(fss[:, i:i + 1], fss[:, i:i + 1], AF.Sqrt, r=[("fss", i)], w=[("fss", i)], scale=1.0 / D, bias=EPS)
        S.op("dve", lambda e, i=i: e.reciprocal(out=fss[:, i:i + 1], in_=fss[:, i:i + 1]), r=[("fss", i)], w=[("fss", i)])
        stt(yo[i], ytm[i], fss[:, i:i + 1], gf_bc, ALU.mult, ALU.mult, r=ky + [("fss", i), ("gf_bc",)], w=[(f"yo{i}",)])
        dma("sp", y_d[128 * tt_:128 * tt_ + 128, :], yo[i], r=[(f"yo{i}",)], w=[("y_d", tt_)], group="y", slot=i)

    S.finalize()
    bad = sorted({k[0] for k in S.tiles if isinstance(k, tuple)} - M.all_names - {"psum", "dbg", "y_d"})
    assert not bad, f"tile keys without buffer: {bad}"
    sems = {e: es.enter_context(nc.semaphore("s_" + e)) for e in Sched.ENGS}
    dsems = [es.enter_context(nc.semaphore(f"d_{i}")) for i in range(S.n_dma_sems)]
    with nc.Block() as block:
        @block.tensor
        def _(e):
            S.emit("pe", e, sems, dsems)

        @block.scalar
        def _(e):
            S.emit("act", e, sems, dsems)

        @block.vector
        def _(e):
            S.emit("dve", e, sems, dsems)

        @block.gpsimd
        def _(e):
            S.emit("pool", e, sems, dsems)

        @block.sync
        def _(e):
            S.emit("sp", e, sems, dsems, final_wait_all_dma=True)
    es.close()
    return nc


def _consts():
    ident = np.eye(128, dtype=np.float32)
    sel = np.zeros((128, 64, 128), dtype=np.float32)
    for a in range(8):
        for b in range(8):
            for ch in range(16):
                sel[16 * a + ch, a * 8 + b, 16 * b + ch] = 1.0
    mask = np.zeros((128, 128), dtype=np.float32)
    for i in range(8):
        for j in range(i, 8):
            mask[16 * i:16 * i + 16, 16 * j:16 * j + 16] = 1.0
    kvec = np.array([7, 6, 5, 4, 3, 2, 1, 0, 1, 2, 3, 4, 5, 6, 7, 8, -7, -6, -5, -4, -3, -2, -1, 0], dtype=np.float32)
    qvec = np.arange(1, NCP + 1, dtype=np.float32)
    return {"c_ident": ident, "c_sel": sel.reshape(128, 64 * 128), "c_mask": mask, "c_kvec": kvec, "c_qvec": qvec}


_NC_CACHE = {}


def kernel(**inputs):
    f = lambda a: np.ascontiguousarray(np.asarray(a, dtype=np.float32))
    x = f(inputs["x"])
    meta = f(inputs["meta"])
    shared = {
        "w_in": f(inputs["w_in"])[0], "w_glu": f(inputs["w_glu"])[0], "w_conv_out": f(inputs["w_conv_out"])[0],
        "w_out": f(inputs["w_out"])[0], "w_ff1": f(inputs["w_ff1"])[0], "w_ff2": f(inputs["w_ff2"])[0],
        "lam_re": f(inputs["lam_re"])[0], "lam_im": f(inputs["lam_im"])[0], "log_dt": f(inputs["log_dt"])[0],
        "b_re": f(inputs["b_re"])[0], "b_im": f(inputs["b_im"])[0], "c_re": f(inputs["c_re"])[0], "c_im": f(inputs["c_im"])[0],
        "d_skip": f(inputs["d_skip"])[0], "conv_w": f(inputs["conv_w"])[0], "conv_b": f(inputs["conv_b"])[0],
        "conv_ln_g": f(inputs["conv_ln_g"])[0], "conv_ln_b": f(inputs["conv_ln_b"])[0],
        "norm_mix_g": f(inputs["norm_mix_g"])[0], "norm_ffn_g": f(inputs["norm_ffn_g"])[0], "norm_f_g": f(inputs["norm_f_g"]),
    }
    shared.update(_consts())
    B = x.shape[0]
    in_maps = []
    for b in range(B):
        for h in range(2):
            if h == 0:
                xp = np.concatenate([np.zeros((NP - 16, D), np.float32), meta], axis=0)
                xm = x[b, 0:NM]
            else:
                xp = np.concatenate([meta, x[b, 0:NM]], axis=0)
                xm = x[b, NM:2 * NM]
            m = dict(shared)
            m["xp"] = np.ascontiguousarray(xp)
            m["xm"] = np.ascontiguousarray(xm)
            in_maps.append(m)
    if "nc" not in _NC_CACHE:
        _NC_CACHE["nc"] = build_program()
    nc = _NC_CACHE["nc"]
    res = run_bass_kernel_spmd(nc, in_maps, core_ids=list(range(8)))
    if DEBUG:
        _NC_CACHE["res"] = res
    out = np.empty((B, 2 * NM, D), dtype=np.float32)
    for b in range(B):
        for h in range(2):
            out[b, h * NM:(h + 1) * NM] = res.results[2 * b + h]["y"]
    return out
```

```python
import math
from contextlib import ExitStack
import numpy as np
import concourse.bass as bass
import concourse.mybir as mybir
from concourse.bass_utils import run_bass_kernel_spmd

F32 = mybir.dt.float32
BF16 = mybir.dt.bfloat16
I32 = mybir.dt.int32
AF = mybir.ActivationFunctionType
ALU = mybir.AluOpType

D = 2048
NP = 1040
NM = 1024
NCP = NP // 8
NCM = NM // 8
EPS = 1e-6
TWO_PI = 2.0 * math.pi
ARENA_BYTES = 200 * 1024
DEBUG = False
SAME_ENGINE_MIN_DIST = 2


class Op:
    __slots__ = ("eng", "fn", "deps", "semkey", "seq", "val", "need", "is_dma", "is_mm")


class Sched:
    ENGS = ("pe", "act", "dve", "pool", "sp")

    GROUPS = (("w", 8), ("x", 3), ("y", 2), ("misc", 12), ("dbg", 2))

    def __init__(self):
        self.ops = {e: [] for e in self.ENGS}
        self.tiles = {}
        self.buf_fence = {}
        self.grp = {}
        base = 0
        for name, n in self.GROUPS:
            self.grp[name] = [base, n, 0]
            base += n
        self.n_dma_sems = base
        self.dma_last = [None] * base
        self.dma_count = [0] * base
        self.all_ops = []

    @staticmethod
    def _merge(dst, src):
        for k, o in src.items():
            cur = dst.get(k)
            if cur is None or o.seq > cur.seq:
                dst[k] = o

    def _tile(self, key):
        t = self.tiles.get(key)
        if t is None:
            f = self.buf_fence.get(key[0]) if isinstance(key, tuple) else None
            t = [dict(f) if f else {}, {}]
            self.tiles[key] = t
        return t

    def _add(self, eng, fn, r, w, is_dma=False, is_mm=False, group="misc", slot=None):
        op = Op()
        op.eng = eng
        op.fn = fn
        op.is_dma = is_dma
        op.is_mm = is_mm
        op.need = is_dma
        op.val = None
        deps = {}
        for k in r:
            self._merge(deps, self._tile(k)[0])
        for k in w:
            t = self._tile(k)
            self._merge(deps, t[0])
            self._merge(deps, t[1])
        if is_dma:
            gi = self.grp[group]
            if slot is None:
                slot = gi[2]
                gi[2] = (gi[2] + 1) % gi[1]
            s = gi[0] + (slot % gi[1])
            prev = self.dma_last[s]
            if prev is not None:
                self._merge(deps, {prev.semkey: prev})
            self.dma_count[s] += 1
            op.semkey = ("dma", s)
            op.seq = self.dma_count[s]
            op.val = 16 * self.dma_count[s]
            self.dma_last[s] = op
        else:
            op.semkey = eng
            op.seq = len(self.ops[eng])
        if eng == "pe" and "pe" in deps:
            del deps["pe"]
        if (not is_dma) and eng in ("dve", "act") and eng in deps and len(self.ops[eng]) - deps[eng].seq >= SAME_ENGINE_MIN_DIST:
            del deps[eng]
        op.deps = list(deps.values())
        for d in op.deps:
            d.need = True
        me = {op.semkey: op}
        for k in w:
            t = self._tile(k)
            t[0] = dict(me)
            t[1] = {}
        for k in r:
            self._merge(self._tile(k)[1], me)
        self.ops[eng].append(op)
        self.all_ops.append(op)
        return op

    def op(self, eng, fn, r=(), w=(), mm=False):
        return self._add(eng, fn, r, w, is_mm=mm)

    def dma(self, eng, fn, r=(), w=(), group="misc", slot=None):
        return self._add(eng, fn, r, w, is_dma=True, group=group, slot=slot)

    def retire(self, old_names, new_name):
        f = self.buf_fence.setdefault(new_name, {})
        for key, t in self.tiles.items():
            if isinstance(key, tuple) and key[0] in old_names:
                self._merge(f, t[0])
                self._merge(f, t[1])
        for n in old_names:
            if n in self.buf_fence and n != new_name:
                self._merge(f, self.buf_fence[n])
        for key, t in self.tiles.items():
            if isinstance(key, tuple) and key[0] == new_name:
                self._merge(t[0], f)

    def finalize(self):
        for e in self.ENGS:
            c = 0
            for o in self.ops[e]:
                if not o.is_dma:
                    if o.need:
                        c += 1
                        o.val = c

    def emit(self, eng_name, eng, sems, dma_sems, final_wait_all_dma=False):
        seen = {}
        for o in self.ops[eng_name]:
            for d in o.deps:
                k = d.semkey
                if seen.get(k, 0) >= d.val:
                    continue
                seen[k] = d.val
                sem = dma_sems[k[1]] if isinstance(k, tuple) else sems[k]
                eng.wait_ge(sem, d.val)
            ins = o.fn(eng)
            if o.is_dma:
                ins.then_inc(dma_sems[o.semkey[1]], 16)
            elif o.need:
                ins.then_inc(sems[o.semkey], 1)
        if final_wait_all_dma:
            for s in range(self.n_dma_sems):
                if self.dma_count[s] > 0:
                    eng.wait_ge(dma_sems[s], 16 * self.dma_count[s])


class Mem:
    def __init__(self, arena, nbytes, sched):
        self.arena = arena
        self.nbytes = nbytes
        self.live = {}
        self.dead = []
        self.S = sched

    def alloc(self, name, shape, dt):
        n = int(np.prod(shape[1:]))
        esz = 4 if dt in (F32, I32) else 2
        nb = (n * esz + 63) // 64 * 64
        ivs = sorted(self.live.values())
        lo = 0
        for (a, b) in ivs:
            if a - lo >= nb:
                break
            lo = max(lo, b)
        assert lo + nb <= self.nbytes, f"SBUF arena overflow allocating {name} ({nb} B); live={self.live}"
        hi = lo + nb
        self.live[name] = (lo, hi)
        self.all_names = getattr(self, "all_names", set())
        self.all_names.add(name)
        olds = [nm for (a, b, nm) in self.dead if a < hi and b > lo]
        if olds:
            self.S.retire(set(olds), name)
        v = self.arena[0:shape[0], lo // 4:(lo + n * esz + 3) // 4]
        if dt != F32:
            v = v.bitcast(dt)
        v = v[:, 0:n]
        if len(shape) == 3:
            v = v.rearrange("p (a b) -> p a b", a=shape[1])
        elif len(shape) == 4:
            v = v.rearrange("p (a b c) -> p a b c", a=shape[1], b=shape[2])
        return v

    def free(self, *names):
        for name in names:
            lo, hi = self.live.pop(name)
            self.dead.append((lo, hi, name))


def build_program():
    nc = bass.Bass("TRN2", target_bir_lowering=False)

    def din(name, shape):
        return nc.dram_tensor(name, list(shape), F32, kind="ExternalInput").ap()

    xp_d = din("xp", [NP, D])
    xm_d = din("xm", [NM, D])
    w_in = din("w_in", [D, 7168])
    w_glu = din("w_glu", [1024, 4096])
    w_co = din("w_conv_out", [1024, 2048])
    w_out = din("w_out", [D, D])
    w_ff1 = din("w_ff1", [D, 8192])
    w_ff2 = din("w_ff2", [8192, D])
    lam_re_d = din("lam_re", [64, 64])
    lam_im_d = din("lam_im", [64, 64])
    log_dt_d = din("log_dt", [64])
    b_re_d = din("b_re", [64, 64, 16])
    b_im_d = din("b_im", [64, 64, 16])
    c_re_d = din("c_re", [64, 16, 64])
    c_im_d = din("c_im", [64, 16, 64])
    d_skip_d = din("d_skip", [1024])
    conv_w_d = din("conv_w", [31, 1024])
    conv_b_d = din("conv_b", [1024])
    ln_g_d = din("conv_ln_g", [1024])
    ln_b_d = din("conv_ln_b", [1024])
    g_mix_d = din("norm_mix_g", [D])
    g_ffn_d = din("norm_ffn_g", [D])
    g_f_d = din("norm_f_g", [D])
    ident_d = din("c_ident", [128, 128])
    sel_d = din("c_sel", [128, 64 * 128])
    mask_d = din("c_mask", [128, 128])
    kvec_d = din("c_kvec", [24])
    qvec_d = din("c_qvec", [NCP])
    y_d = nc.dram_tensor("y", [NM, D], F32, kind="ExternalOutput").ap()

    S = Sched()
    es = ExitStack()
    dbg_outs = {}

    def dump(name, ap, r):
        if not DEBUG:
            return
        shp = list(ap.shape)
        dt = ap.dtype
        d = nc.dram_tensor("dbg_" + name, shp, dt, kind="ExternalOutput").ap()
        dbg_outs[name] = d
        S.dma("sp", lambda e: e.dma_start(out=d, in_=ap), r=r, w=[("dbg", name)], group="dbg")

    arena = es.enter_context(nc.sbuf_tensor("arena", [128, ARENA_BYTES // 4], F32))
    M = Mem(arena, ARENA_BYTES, S)
    banks = [es.enter_context(nc.psum_tensor(f"psb{i}", [128, 512], F32)) for i in range(8)]
    bank_rr = [0]

    def psum(which=None):
        if which is None:
            which = bank_rr[0]
            bank_rr[0] = (bank_rr[0] + 1) % 8
        return banks[which], ("psum", which)

    def psum_from(lst, ctr):
        b = lst[ctr[0] % len(lst)]
        ctr[0] += 1
        return banks[b], ("psum", b)

    def mm(out, lhsT, rhs, start, stop, r, w):
        S.op("pe", lambda e: e.matmul(out, lhsT=lhsT, rhs=rhs, start=start, stop=stop), r=r, w=w, mm=True)

    def tr(out, in_, ident, r, w):
        S.op("pe", lambda e: e.transpose(out, in_, ident), r=r, w=w, mm=True)

    def act(out, in_, func, r, w, scale=1.0, bias=None, accum=None):
        def fn(e):
            kw = {}
            if bias is not None:
                kw["bias"] = bias
            if accum is not None:
                kw["accum_out"] = accum
            return e.activation(out=out, in_=in_, func=func, scale=scale, **kw)
        S.op("act", fn, r=r, w=w)

    def tt(eng, out, in0, in1, op, r, w):
        S.op(eng, lambda e: e.tensor_tensor(out=out, in0=in0, in1=in1, op=op), r=r, w=w)

    def ts(eng, out, in0, s1, s2, op0, op1, r, w):
        if op1 is None:
            S.op(eng, lambda e: e.tensor_scalar(out=out, in0=in0, scalar1=s1, scalar2=None, op0=op0), r=r, w=w)
        else:
            S.op(eng, lambda e: e.tensor_scalar(out=out, in0=in0, scalar1=s1, scalar2=s2, op0=op0, op1=op1), r=r, w=w)

    def stt(out, in0, scalar, in1, op0, op1, r, w):
        S.op("dve", lambda e: e.scalar_tensor_tensor(out=out, in0=in0, scalar=scalar, in1=in1, op0=op0, op1=op1), r=r, w=w)

    def cp(eng, out, in_, r, w):
        if eng == "act":
            act(out, in_, AF.Copy, r, w)
        else:
            S.op(eng, lambda e: e.tensor_copy(out=out, in_=in_), r=r, w=w)

    def dma(eng, out, in_, r, w, slow=False, group="misc", slot=None):
        if slow:
            S.dma(eng, lambda e: e.dma_start(out=out, in_=in_, allow_slow_non_contiguous=True), r=r, w=w, group=group, slot=slot)
        else:
            S.dma(eng, lambda e: e.dma_start(out=out, in_=in_), r=r, w=w, group=group, slot=slot)

    def memset(eng, ap, val, w):
        S.op(eng, lambda e: e.memset(ap, val), r=(), w=w)

    evac_rr = [0]

    def evac_eng():
        evac_rr[0] += 1
        return "act" if evac_rr[0] % 2 else "dve"

    identf = M.alloc("identf", [128, 128], F32)
    identb = M.alloc("identb", [128, 128], BF16)
    onesb = M.alloc("onesb", [128, 128], BF16)
    sel = M.alloc("sel", [128, 64, 128], BF16)
    gmix_bc = M.alloc("gmix_bc", [128, D], F32)
    gffn = M.alloc("gffn", [128, 16], F32)
    convb = M.alloc("convb", [128, 8], F32)
    lng = M.alloc("lng", [128, 8], F32)
    lnb = M.alloc("lnb", [128, 8], F32)
    convw = M.alloc("convw", [128, 8, 31], F32)

    dma("sp", identf, ident_d, r=(), w=[("identf",)])
    cp("dve", identb, identf, r=[("identf",)], w=[("identb",)])
    memset("dve", onesb, 1.0, w=[("onesb",)])
    def late_loads():
        dma("pool", sel, sel_d.rearrange("p (a b) -> p a b", b=128), r=(), w=[("sel",)], group="w", slot=7)
        dma("sp", gmix_bc, g_mix_d.partition_broadcast(128), r=(), w=[("gmix_bc",)])
        dma("act", gffn, g_ffn_d.rearrange("(c p) -> p c", p=128), r=(), w=[("gffn",)], slow=True)
        dma("act", convb, conv_b_d.rearrange("(c p) -> p c", p=128), r=(), w=[("convb",)], slow=True)
        dma("act", lng, ln_g_d.rearrange("(c p) -> p c", p=128), r=(), w=[("lng",)], slow=True)
        dma("act", lnb, ln_b_d.rearrange("(c p) -> p c", p=128), r=(), w=[("lnb",)], slow=True)

        cw_raw = M.alloc("cw_raw", [31, 1024], F32)
        dma("sp", cw_raw, conv_w_d, r=(), w=[("cw_raw",)])
        for half in range(2):
            ps, pk = psum()
            for jj in range(4):
                j = half * 4 + jj
                tr(ps[:, jj * 32:jj * 32 + 31], cw_raw[0:31, j * 128:(j + 1) * 128], identf[0:31, 0:31],
                   r=[("cw_raw",), ("identf",)], w=[pk])
            cp("dve", convw[:, half * 4:half * 4 + 4, :],
               ps[:, 0:128].rearrange("p (a b) -> p a b", a=4)[:, :, 0:31], r=[pk], w=[("convw",)])
        M.free("cw_raw")

    lamre = M.alloc("lamre", [128, 32], F32)
    lamim = M.alloc("lamim", [128, 32], F32)
    dtt = M.alloc("dtt", [128, 32], F32)
    bre = M.alloc("bre", [128, 32, 16], F32)
    bim = M.alloc("bim", [128, 32, 16], F32)
    cre = M.alloc("cre", [128, 32, 16], F32)
    cim = M.alloc("cim", [128, 32, 16], F32)
    kv = M.alloc("kv", [128, 24], F32)
    dsk = M.alloc("dsk", [128, 64], F32)
    maskt = M.alloc("maskt", [128, 128], F32)
    phiw = M.alloc("phiw", [128, 32], F32)
    qv = M.alloc("qv", [128, NCP], F32)
    rho = M.alloc("rho", [128, 32], F32)
    fcos = M.alloc("fcos", [128, 32], F32)
    fsin = M.alloc("fsin", [128, 32], F32)
    ZW = M.alloc("ZW", [128, 32, 2, 128], BF16)
    KW = M.alloc("KW", [128, 64, 128], BF16)
    OW = M.alloc("OW", [128, 32, 2, 128], BF16)

    lam_raw = M.alloc("lam_raw", [32, 3, 128], F32)
    ldt = M.alloc("ldt", [32, 2], F32)
    dma("sp", lam_raw[:, 0, :], lam_re_d.rearrange("(p g2) n -> p (g2 n)", g2=2), r=(), w=[("lam_raw", 0)])
    dma("act", lam_raw[:, 1, :], lam_im_d.rearrange("(p g2) n -> p (g2 n)", g2=2), r=(), w=[("lam_raw", 1)])
    dma("sp", ldt, log_dt_d.rearrange("(p g2) -> p g2", g2=2), r=(), w=[("ldt",)])
    dma("sp", kv, kvec_d.partition_broadcast(128), r=(), w=[("kv",)])
    cp("dve", lam_raw[:, 2, :].rearrange("p (g2 n) -> p g2 n", g2=2), ldt.unsqueeze(2).to_broadcast([32, 2, 64]),
       r=[("ldt",)], w=[("lam_raw", 2)])
    ps, pk = psum()
    for w_ in range(3):
        tr(ps[:, 32 * w_:32 * w_ + 32], lam_raw[0:32, w_, :], identf[0:32, 0:32], r=[("lam_raw", w_), ("identf",)], w=[pk])
    cp("dve", lamre, ps[:, 0:32], r=[pk], w=[("lamre",)])
    cp("dve", lamim, ps[:, 32:64], r=[pk], w=[("lamim",)])
    cp("dve", dtt, ps[:, 64:96], r=[pk], w=[("dtt", 0), ("dtt", 1)])
    M.free("lam_raw", "ldt")
    dma("sp", bre, b_re_d.rearrange("(p g2) n h -> (g2 n) p h", g2=2), r=(), w=[("bre",)])
    dma("act", bim, b_im_d.rearrange("(p g2) n h -> (g2 n) p h", g2=2), r=(), w=[("bim",)])
    for nm, src, dst in (("cre", c_re_d, cre), ("cim", c_im_d, cim)):
        raw = M.alloc(nm + "_raw", [64, 8, 128], F32)
        sv = src.rearrange("(b q g2) h n -> q h b g2 n", q=4, g2=2)
        for q4 in range(4):
            for g2 in range(2):
                dma("sp" if g2 == 0 else "act", raw[16 * q4:16 * q4 + 16, :, 64 * g2:64 * g2 + 64], sv[q4][:, :, g2, :],
                    r=(), w=[(nm + "_raw", q4, g2)])
        for b4 in range(2):
            ps, pk = psum()
            for bb in range(4):
                b = b4 * 4 + bb
                tr(ps[:, bb * 64:(bb + 1) * 64], raw[0:64, b, :], identf[0:64, 0:64],
                   r=[(nm + "_raw", q, g2) for q in range(4) for g2 in range(2)] + [("identf",)], w=[pk])
            cp("dve", dst[:, 16 * b4:16 * b4 + 16, :].rearrange("p a h -> p (a h)"), ps[:, 0:256], r=[pk], w=[(nm,)])
        M.free(nm + "_raw")

    dma("act", qv, qvec_d.partition_broadcast(128), r=(), w=[("qv",)])
    dma("act", maskt, mask_d, r=(), w=[("maskt",)])
    dsv = d_skip_d.rearrange("(g ch) -> ch g", ch=16)
    for i in range(8):
        dma("sp" if i % 2 == 0 else "act", dsk[16 * i:16 * i + 16, :], dsv, r=(), w=[("dsk", i)], slow=True)

    a_t = M.alloc("a_t", [128, 32], F32)
    th_t = M.alloc("th_t", [128, 32], F32)
    mag = M.alloc("mag", [128, 32, 24], F32)
    ang = M.alloc("ang", [128, 32, 24], F32)
    angi = M.alloc("angi", [128, 32, 24], I32)
    angf = M.alloc("angf", [128, 32, 24], F32)
    cosv = M.alloc("cosv", [128, 32, 24], F32)
    sinv = M.alloc("sinv", [128, 32, 24], F32)
    LPre = M.alloc("LPre", [128, 32, 24], F32)
    LPim = M.alloc("LPim", [128, 32, 24], F32)

    act(dtt, dtt, AF.Exp, r=[("dtt", 0), ("dtt", 1)], w=[("dtt", 0), ("dtt", 1)])
    tt("dve", a_t, lamre, dtt, ALU.mult, r=[("lamre",), ("dtt", 0), ("dtt", 1)], w=[("a_t",)])
    tt("dve", th_t, lamim, dtt, ALU.mult, r=[("lamim",), ("dtt", 0), ("dtt", 1)], w=[("th_t",)])
    kvb = kv.unsqueeze(1).to_broadcast([128, 32, 24])
    tt("dve", mag, a_t.unsqueeze(2).to_broadcast([128, 32, 24]), kvb, ALU.mult, r=[("a_t",), ("kv",)], w=[("mag",)])
    act(mag, mag, AF.Exp, r=[("mag",)], w=[("mag",)])
    tt("dve", ang, th_t.unsqueeze(2).to_broadcast([128, 32, 24]), kvb, ALU.mult, r=[("th_t",), ("kv",)], w=[("ang",)])

    def range_reduce(dst, src, shift, keys_r, key_w):
        ts("dve", angf, src, 1.0 / TWO_PI, shift / TWO_PI, ALU.mult, ALU.add, r=keys_r, w=[("angf",)])
        cp("dve", angi, angf, r=[("angf",)], w=[("angi",)])
        cp("dve", angf, angi, r=[("angi",)], w=[("angf",)])
        stt(dst, angf, -TWO_PI, src, ALU.mult, ALU.add, r=[("angf",)] + keys_r, w=[key_w])
        ts("dve", dst, dst, shift, math.pi, ALU.add, ALU.min, r=[key_w], w=[key_w])
        ts("dve", dst, dst, -math.pi, None, ALU.max, None, r=[key_w], w=[key_w])

    range_reduce(sinv, ang, 0.0, [("ang",)], ("sinv",))
    cp("dve", phiw, sinv[:, :, 15], r=[("sinv",)], w=[("phiw",)])
    act(sinv, sinv, AF.Sin, r=[("sinv",)], w=[("sinv",)])
    range_reduce(cosv, ang, math.pi / 2, [("ang",)], ("cosv",))
    act(cosv, cosv, AF.Sin, r=[("cosv",)], w=[("cosv",)])
    tt("dve", LPre, mag, cosv, ALU.mult, r=[("mag",), ("cosv",)], w=[("LPre",)])
    tt("dve", LPim, mag, sinv, ALU.mult, r=[("mag",), ("sinv",)], w=[("LPim",)])
    cp("dve", rho, mag[:, :, 15], r=[("mag",)], w=[("rho",)])
    cp("dve", fcos, cosv[:, :, 15], r=[("cosv",)], w=[("fcos",)])
    cp("dve", fsin, sinv[:, :, 15], r=[("sinv",)], w=[("fsin",)])

    dump("dtt", dtt, [("dtt", 0), ("dtt", 1)])
    dump("lamre", lamre, [("lamre",)])
    dump("a_t", a_t, [("a_t",)])
    dump("mag", mag, [("mag",)])
    dump("ang", ang, [("ang",)])
    dump("cosv", cosv, [("cosv",)])
    dump("sinv", sinv, [("sinv",)])
    dump("kv", kv, [("kv",)])
    t1 = M.alloc("t1", [128, 32], F32)
    t2 = M.alloc("t2", [128, 32], F32)
    t3 = M.alloc("t3", [128, 32], F32)
    cfr = M.alloc("cfr", [128, 32], F32)
    cfi = M.alloc("cfi", [128, 32], F32)
    kS = [("t1",), ("t2",), ("t3",)]
    ts("dve", t1, LPre[:, :, 8], -1.0, None, ALU.add, None, r=[("LPre",)], w=[("t1",)])
    tt("dve", t2, lamre, lamre, ALU.mult, r=[("lamre",)], w=[("t2",)])
    tt("dve", t3, lamim, lamim, ALU.mult, r=[("lamim",)], w=[("t3",)])
    tt("dve", t2, t2, t3, ALU.add, r=[("t2",), ("t3",)], w=[("t2",)])
    S.op("dve", lambda e: e.reciprocal(out=t2, in_=t2), r=[("t2",)], w=[("t2",)])
    tt("dve", cfr, t1, lamre, ALU.mult, r=[("t1",), ("lamre",)], w=[("cfr",)])
    tt("dve", t3, LPim[:, :, 8], lamim, ALU.mult, r=[("LPim",), ("lamim",)], w=[("t3",)])
    tt("dve", cfr, cfr, t3, ALU.add, r=[("cfr",), ("t3",)], w=[("cfr",)])
    tt("dve", cfr, cfr, t2, ALU.mult, r=[("cfr",), ("t2",)], w=[("cfr",)])
    tt("dve", cfi, LPim[:, :, 8], lamre, ALU.mult, r=[("LPim",), ("lamre",)], w=[("cfi",)])
    tt("dve", t3, t1, lamim, ALU.mult, r=[("t1",), ("lamim",)], w=[("t3",)])
    tt("dve", cfi, cfi, t3, ALU.subtract, r=[("cfi",), ("t3",)], w=[("cfi",)])
    tt("dve", cfi, cfi, t2, ALU.mult, r=[("cfi",), ("t2",)], w=[("cfi",)])

    bbr = M.alloc("bbr", [128, 32, 16], F32)
    bbi = M.alloc("bbi", [128, 32, 16], F32)
    tb = M.alloc("tb", [128, 32, 16], F32)
    cfrb = cfr.unsqueeze(2).to_broadcast([128, 32, 16])
    cfib = cfi.unsqueeze(2).to_broadcast([128, 32, 16])
    tt("dve", bbr, bre, cfrb, ALU.mult, r=[("bre",), ("cfr",)], w=[("bbr",)])
    tt("dve", tb, bim, cfib, ALU.mult, r=[("bim",), ("cfi",)], w=[("tb",)])
    tt("dve", bbr, bbr, tb, ALU.subtract, r=[("bbr",), ("tb",)], w=[("bbr",)])
    tt("dve", bbi, bim, cfrb, ALU.mult, r=[("bim",), ("cfr",)], w=[("bbi",)])
    tt("dve", tb, bre, cfib, ALU.mult, r=[("bre",), ("cfi",)], w=[("tb",)])
    tt("dve", bbi, bbi, tb, ALU.add, r=[("bbi",), ("tb",)], w=[("bbi",)])

    M.free("a_t", "th_t", "mag", "ang", "angi", "angf", "cosv", "sinv", "t1", "t2", "t3")
    Pre = M.alloc("Pre", [128, 32, 128], F32)
    Pim = M.alloc("Pim", [128, 32, 128], F32)
    CRr = M.alloc("CRr", [128, 32, 128], F32)
    CRi = M.alloc("CRi", [128, 32, 128], F32)
    tmpA = M.alloc("tmpA", [128, 32, 128], F32)
    sh4 = [128, 32, 8, 16]

    def v4(t):
        return t.rearrange("p a (b c) -> p a b c", b=8)

    def lpb(t, i0):
        return t[:, :, i0:i0 + 8].unsqueeze(3).to_broadcast(sh4)

    def chb(t):
        return t.unsqueeze(2).to_broadcast(sh4)

    def cmul(dst_re, dst_im, ar, ai, br_, bi_, keys_a, keys_b, kre, kim, neg_im=False, dre4=None, dim4=None):
        dre4 = v4(dst_re) if dre4 is None else dre4
        dim4 = v4(dst_im) if dim4 is None else dim4
        tA = v4(tmpA)
        tt("dve", tA, ar, br_, ALU.mult, r=keys_a + keys_b, w=[("tmpA",)])
        tt("dve", dim4, ai, bi_, ALU.mult, r=keys_a + keys_b, w=[kim])
        tt("dve", dre4, tA, dim4, ALU.subtract, r=[("tmpA",), kim], w=[kre])
        tt("dve", tA, ar, bi_, ALU.mult, r=keys_a + keys_b, w=[("tmpA",)])
        tt("dve", dim4, ai, br_, ALU.mult, r=keys_a + keys_b + [kre], w=[kim])
        if neg_im:
            stt(dim4, tA, -1.0, dim4, ALU.mult, ALU.subtract, r=[("tmpA",), kim], w=[kim])
        else:
            tt("dve", dim4, tA, dim4, ALU.add, r=[("tmpA",), kim], w=[kim])

    kLP = [("LPre",), ("LPim",)]
    cmul(Pre, Pim, lpb(LPre, 0), lpb(LPim, 0), chb(bbr), chb(bbi), kLP, [("bbr",), ("bbi",)], ("Pre",), ("Pim",))
    cmul(CRr, CRi, lpb(LPre, 8), lpb(LPim, 8), chb(cre), chb(cim), kLP, [("cre",), ("cim",)], ("CRr",), ("CRi",),
         neg_im=True)
    cp("dve", OW[:, :, 0, :], CRr, r=[("CRr",)], w=[("OW",)])
    cp("dve", OW[:, :, 1, :], CRi, r=[("CRi",)], w=[("OW",)])
    cmul(CRr, CRi, lpb(LPre, 16), lpb(LPim, 16), chb(cre), chb(cim), kLP, [("cre",), ("cim",)], ("CRr",), ("CRi",),
         neg_im=True)

    late_loads()
    for p in range(32):
        ps, pk = psum()
        tr(ps[:, 0:128], Pre[:, p, :], identf, r=[("Pre",), ("identf",)], w=[pk])
        tr(ps[:, 128:256], Pim[:, p, :], identf, r=[("Pim",), ("identf",)], w=[pk])
        cp(evac_eng(), ZW[:, p, :, :].rearrange("p a b -> p (a b)"), ps[:, 0:256], r=[pk], w=[("ZW",)])
    kwtmp = M.alloc("kwtmp", [128, 4, 128], F32)
    for p in range(32):
        for g2 in range(2):
            g = 2 * p + g2
            ps, pk = psum()
            sl = slice(64 * g2, 64 * g2 + 64)
            mm(ps[:, 0:128], Pre[sl, p, :], CRr[sl, p, :], True, False, r=[("Pre",), ("CRr",)], w=[pk])
            mm(ps[:, 0:128], Pim[sl, p, :], CRi[sl, p, :], False, True, r=[("Pim",), ("CRi",)], w=[pk])
            kt = kwtmp[:, g % 4, :]
            tt("dve", kt, ps[:, 0:128], maskt, ALU.mult, r=[pk, ("maskt",)], w=[("kwtmp", g % 4)])
            stt(KW[:, g, :], identf, dsk[:, g:g + 1], kt, ALU.mult, ALU.add,
                r=[("identf",), ("kwtmp", g % 4)] + [("dsk", i) for i in range(8)], w=[("KW", g)])
    M.free("LPre", "LPim", "cfr", "cfi",
           "bbr", "bbi", "tb", "Pre", "Pim", "CRr", "CRi", "tmpA", "kwtmp",
           "lamre", "lamim", "dtt", "bre", "bim", "cre", "cim", "kv", "maskt")

    NW = 3
    wslots = []
    w_rr = [0]

    def alloc_wslots(n=NW):
        wslots.clear()
        wslots.extend(M.alloc(f"wslot{i}", [128, 16, 256], BF16) for i in range(n))

    def free_wslots():
        M.free(*[f"wslot{i}" for i in range(len(wslots))])

    alloc_wslots(2)

    def wtile(W, k0, KC, c0):
        i = w_rr[0] % len(wslots)
        w_rr[0] += 1
        src = W[k0:k0 + 128 * KC, c0:c0 + 256].rearrange("(kc p) m -> p kc m", p=128)
        dma("pool", wslots[i][:, 0:KC, :], src, r=(), w=[(f"wslot{i}",)], group="w", slot=i)
        return wslots[i], (f"wslot{i}",)

    def proj(wt, wk, KC, mo, rhs_list, r_keys):
        outs = []
        for (rf, n) in rhs_list:
            ps, pk = psum()
            outs.append((ps, pk, n))
        for kc in range(KC):
            for ri, ((rf, n), (ps, pk, _)) in enumerate(zip(rhs_list, outs)):
                rk = r_keys(kc, ri) if callable(r_keys) else r_keys
                mm(ps[:, 0:n], wt[:, kc, mo:mo + 128], rf(kc), kc == 0, kc == KC - 1, r=[wk] + rk, w=[pk])
        return outs

    xin, xnb, junkl = [], [], []
    ssq = M.alloc("ssq", [128, 4], F32)
    ld_rr = [0]

    def alloc_load_bufs():
        xin.clear(); xnb.clear()
        xin.extend(M.alloc(f"xin{i}", [128, D], F32) for i in range(3))
        xnb.extend(M.alloc(f"xnb{i}", [128, D], BF16) for i in range(2))

    def free_load_bufs():
        M.free("xin0", "xin1", "xin2", "xnb0", "xnb1")

    alloc_load_bufs()

    def load_tokens(src, ntok, hb, hbname, xres=None):
        tiles = []
        t0 = 0
        while t0 < ntok:
            nt = min(128, ntok - t0)
            tiles.append((t0, nt, ld_rr[0]))
            ld_rr[0] += 1
            t0 += nt

        def stage_a(t0, nt, k):
            i, j = k % 3, k % 2
            kx, kn, ks = (f"xin{i}",), (f"xnb{j}",), ("ssq", j)
            dma("sp", xin[i][0:nt, :], src[t0:t0 + nt, :], r=(), w=[kx], group="x", slot=i)
            if hb is not None:
                act(xnb[j][0:nt, :], xin[i][0:nt, :], AF.Square, r=[kx], w=[kn, ks], accum=ssq[0:nt, j:j + 1])
                act(ssq[0:nt, j:j + 1], ssq[0:nt, j:j + 1], AF.Sqrt, r=[ks], w=[ks], scale=1.0 / D, bias=EPS)
                S.op("dve", lambda e, j=j, nt=nt: e.reciprocal(out=ssq[0:nt, j:j + 1], in_=ssq[0:nt, j:j + 1]), r=[ks], w=[ks])
                stt(xnb[j][0:nt, :], xin[i][0:nt, :], ssq[0:nt, j:j + 1], gmix_bc[0:nt, :], ALU.mult, ALU.mult,
                    r=[kx, ks, ("gmix_bc",)], w=[kn])

        def stage_b(t0, nt, k):
            i, j = k % 3, k % 2
            kx, kn = (f"xin{i}",), (f"xnb{j}",)
            for q in range(4):
                if hb is not None:
                    ps, pk = psum()
                    psb = ps.bitcast(BF16)
                    for kk in range(4):
                        kc = 4 * q + kk
                        tr(psb[:, kk * 128:kk * 128 + nt], xnb[j][0:nt, kc * 128:(kc + 1) * 128], identb[0:nt, 0:nt],
                           r=[kn, ("identb",)], w=[pk])
                    cp(evac_eng(), hb[:, 4 * q:4 * q + 4, t0:t0 + nt],
                       psb[:, 0:512].rearrange("p (a b) -> p a b", a=4)[:, :, 0:nt], r=[pk], w=[(hbname, t0)])
                if xres is not None:
                    ps2, pk2 = psum()
                    for kk in range(4):
                        kc = 4 * q + kk
                        tr(ps2[:, kk * 128:kk * 128 + nt], xin[i][0:nt, kc * 128:(kc + 1) * 128], identf[0:nt, 0:nt],
                           r=[kx, ("identf",)], w=[pk2])
                    cp(evac_eng(), xres[:, 4 * q:4 * q + 4, t0:t0 + nt],
                       ps2[:, 0:512].rearrange("p (a b) -> p a b", a=4)[:, :, 0:nt], r=[pk2], w=[("xres", t0 // 512)])

        stage_a(*tiles[0])
        for k in range(len(tiles)):
            if k + 1 < len(tiles):
                stage_a(*tiles[k + 1])
            stage_b(*tiles[k])

    def hb_keys(name, ntok):
        return [(name, t) for t in range(0, ntok, 128)]

    u_fm = M.alloc("u_fm", [128, 8, NP + NM], BF16)
    hb_pre = M.alloc("hb_pre", [128, 16, NP], BF16)
    hb_halo = M.alloc("hb_halo", [128, 16, 30], BF16)
    load_tokens(xp_d, NP, hb_pre, "hb_pre")
    kpre = hb_keys("hb_pre", NP)
    cp("dve", hb_halo, hb_pre[:, :, NP - 30:NP], r=kpre, w=[("hb_halo",)])
    for mm_ in range(4):
        wt, wk = wtile(w_in, 0, 16, 256 * mm_)
        for sub in range(2):
            mc = 2 * mm_ + sub
            rl = [(lambda kc, a=a, n=n: hb_pre[:, kc, a:a + n], n) for (a, n) in ((0, 512), (512, 512), (1024, 16))]
            outs = proj(wt, wk, 16, 128 * sub, rl, kpre)
            for (a, n), (ps, pk, _) in zip(((0, 512), (512, 512), (1024, 16)), outs):
                cp(evac_eng(), u_fm[:, mc, 0:NP].rearrange("p (i c) -> p c i", i=8)[:, a // 8:(a + n) // 8, :],
                   ps[:, 0:n].rearrange("p (c i) -> p c i", i=8), r=[pk], w=[("u_fm", mc, "pre")])
    M.free("hb_pre")

    hb = M.alloc("hb", [128, 16, NM], BF16)
    load_tokens(xm_d, NM, hb, "hb")
    M.free("gmix_bc")
    khb = hb_keys("hb", NM)
    dump("hb", hb, khb)
    main_rl = [(lambda kc, a=a: hb[:, kc, a:a + 512], 512) for a in (0, 512)]
    for mm_ in range(4):
        wt, wk = wtile(w_in, 0, 16, 256 * mm_)
        for sub in range(2):
            mc = 2 * mm_ + sub
            outs = proj(wt, wk, 16, 128 * sub, main_rl, khb)
            for a, (ps, pk, _) in zip((0, 512), outs):
                cp(evac_eng(), u_fm[:, mc, NP:NP + NM].rearrange("p (i c) -> p c i", i=8)[:, a // 8:(a + 512) // 8, :],
                   ps[:, 0:512].rearrange("p (c i) -> p c i", i=8), r=[pk], w=[("u_fm", mc, "main")])

    free_load_bufs()
    free_wslots()
    dump("u_fm", u_fm, [("u_fm", mc, w_) for mc in range(8) for w_ in ("pre", "main")])
    dump("ZW", ZW, [("ZW",)])
    dump("KW", KW, [("KW", g) for g in range(64)])
    dump("OW", OW, [("OW",)])
    dump("rho", rho, [("rho",)])
    dump("fcos", fcos, [("fcos",)])
    dump("fsin", fsin, [("fsin",)])

    NCT = NCP + NCM
    u_scr = M.alloc("u_scr", [128, 64, NCT], BF16)
    for g in range(64):
        b, gp = divmod(g, 8)
        ps, pk = psum()
        upre = u_fm[:, b, 0:NP].rearrange("p (i c) -> p i c", i=8)
        umain = u_fm[:, b, NP:NP + NM].rearrange("p (i c) -> p i c", i=8)
        for i in range(8):
            mm(ps[:, 0:NCP], sel[:, gp * 8 + i, :], upre[:, i, :], i == 0, i == 7,
               r=[("sel",), ("u_fm", b, "pre")], w=[pk])
        for i in range(8):
            mm(ps[:, NCP:NCT], sel[:, gp * 8 + i, :], umain[:, i, :], i == 0, i == 7,
               r=[("sel",), ("u_fm", b, "main")], w=[pk])
        cp(evac_eng(), u_scr[:, g, :], ps[:, 0:NCT], r=[pk], w=[("u_scr", g)])
    M.free("u_fm")

    PQ = 4
    Sst = M.alloc("Sst", [128, PQ, 2, NCM], BF16)
    z_scr = M.alloc("z_scr", [128, 2 * PQ, NCM], BF16)
    z_fm = M.alloc("z_fm", [128, 8, NM], BF16)
    NQ = NCP
    Ecs = [M.alloc(f"Ec{i}", [128, PQ, NQ], F32) for i in range(2)]
    Ess = [M.alloc(f"Es{i}", [128, PQ, NQ], F32) for i in range(2)]
    eang = M.alloc("eang", [128, PQ, NQ], F32)
    ekf = M.alloc("ekf", [128, PQ, NQ], F32)
    eki = M.alloc("eki", [128, PQ, NQ], I32)
    Zss = [M.alloc(f"Zs{i}", [128, PQ, 2, NQ], F32) for i in range(2)]
    bufA = M.alloc("bufA", [128, PQ * NQ], F32)
    bufB = M.alloc("bufB", [128, PQ * NQ], F32)
    bufC = M.alloc("bufC", [128, PQ * NQ], F32)
    rtab = M.alloc("rtab", [128, PQ * NQ], F32)
    vinit = M.alloc("vinit", [128, PQ, 2], F32)
    vtmp = M.alloc("vtmp", [128, PQ, 2], F32)

    def v3(buf, n):
        return buf[:, 0:PQ * n].rearrange("p (a b) -> p a b", a=PQ)

    def emit_Z(bt):
        p0 = PQ * bt
        for stage in range(2):
            nz = NCP if stage == 0 else NCM
            c0 = 0 if stage == 0 else NCP
            for pl in range(PQ):
                p = p0 + pl
                for ri in range(2):
                    ps, pk = psum()
                    for g2 in range(2):
                        g = 2 * p + g2
                        mm(ps[64 * g2:64 * g2 + 64, 0:nz], ZW[:, p, ri, 64 * g2:64 * g2 + 64], u_scr[:, g, c0:c0 + nz],
                           True, True, r=[("ZW",), ("u_scr", g)], w=[pk])
                    cp("act", Zss[stage][:, pl, ri, 0:nz], ps[:, 0:nz], r=[pk], w=[(f"Zs{stage}", pl, ri)])

    def emit_E(bt):
        p0 = PQ * bt
        Ec, Es = Ecs[bt % 2], Ess[bt % 2]
        kE = [(f"Ec{bt % 2}",), (f"Es{bt % 2}",)]
        kEc, kEs = kE[0], kE[1]
        tt("dve", eang, phiw[:, p0:p0 + PQ].unsqueeze(2).to_broadcast([128, PQ, NQ]),
           qv.unsqueeze(1).to_broadcast([128, PQ, NQ]), ALU.mult, r=[("phiw",), ("qv",)], w=[("eang",)])
        ts("dve", ekf, eang, 1.0 / TWO_PI, None, ALU.mult, None, r=[("eang",)], w=[("ekf",)])
        cp("dve", eki, ekf, r=[("ekf",)], w=[("eki",)])
        cp("dve", ekf, eki, r=[("eki",)], w=[("ekf",)])
        stt(Es, ekf, -TWO_PI, eang, ALU.mult, ALU.add, r=[("ekf",), ("eang",)], w=[kEs])
        ts("dve", Es, Es, math.pi, -math.pi, ALU.min, ALU.max, r=[kEs], w=[kEs])
        stt(Ec, Es, -1.0, Es, ALU.mult, ALU.max, r=[kEs], w=[kEc])
        act(Ec, Ec, AF.Sin, r=[kEc], w=[kEc], scale=-1.0, bias=math.pi / 2)
        act(Es, Es, AF.Sin, r=[kEs], w=[kEs])

    for bt in range(32 // PQ):
        p0 = PQ * bt
        Ec, Es = Ecs[bt % 2], Ess[bt % 2]
        kE = [(f"Ec{bt % 2}",), (f"Es{bt % 2}",)]
        if bt == 0:
            emit_E(0)
        if bt == 0:
            emit_Z(0)

        for stage in range(2):
            n = NCP if stage == 0 else NCM + 1
            nz = NCP if stage == 0 else NCM
            c0 = 0 if stage == 0 else NCP
            zoff = 0 if stage == 0 else 1
            Zs = Zss[stage]
            kZ = [(f"Zs{stage}", pl, ri) for pl in range(PQ) for ri in range(2)]
            A3, B3, C3, R3 = v3(bufA, n), v3(bufB, n), v3(bufC, n), v3(rtab, n)
            Ecz, Esz = Ec[:, :, 0:nz], Es[:, :, 0:nz]
            Zr, Zi = Zs[:, :, 0, 0:nz], Zs[:, :, 1, 0:nz]
            sl = slice(zoff, zoff + nz)
            cp("dve", R3, rho[:, p0:p0 + PQ].unsqueeze(2).to_broadcast([128, PQ, n]), r=[("rho",)], w=[("rtab",)])
            tt("dve", A3[:, :, sl], Ecz, Zr, ALU.mult, r=kE + kZ, w=[("bufA",)])
            tt("dve", B3[:, :, sl], Esz, Zi, ALU.mult, r=kE + kZ, w=[("bufB",)])
            memset("dve", R3[:, :, 0:1], 0.0, w=[("rtab",)])
            tt("dve", C3[:, :, sl], Ecz, Zi, ALU.mult, r=kE + kZ, w=[("bufC",)])
            tt("dve", Zr, Esz, Zr, ALU.mult, r=kE + kZ, w=kZ)
            tt("dve", A3[:, :, sl], A3[:, :, sl], B3[:, :, sl], ALU.add, r=[("bufA",), ("bufB",)], w=[("bufA",)])
            tt("dve", C3[:, :, sl], C3[:, :, sl], Zr, ALU.subtract, r=[("bufC",)] + kZ, w=[("bufC",)])
            if stage == 1:
                cp("dve", A3[:, :, 0], vinit[:, :, 0], r=[("vinit", 0), ("bufA",)], w=[("bufA",)])
                cp("dve", C3[:, :, 0], vinit[:, :, 1], r=[("vinit", 1), ("bufC",)], w=[("bufC",)])
            fl = slice(0, PQ * n)
            S.op("dve", lambda e, fl=fl: e.tensor_tensor_scan(out=bufA[:, fl], data0=rtab[:, fl], data1=bufA[:, fl],
                                                            initial=0.0, op0=ALU.mult, op1=ALU.add),
                 r=[("rtab",), ("bufA",)], w=[("bufA",)])
            S.op("dve", lambda e, fl=fl: e.tensor_tensor_scan(out=bufC[:, fl], data0=rtab[:, fl], data1=bufC[:, fl],
                                                            initial=0.0, op0=ALU.mult, op1=ALU.add),
                 r=[("rtab",), ("bufC",)], w=[("bufC",)])
            if stage == 0:
                ec, es_ = Ec[:, :, NCP - 1], Es[:, :, NCP - 1]
                vr, vi = A3[:, :, NCP - 1], C3[:, :, NCP - 1]
                tt("dve", vinit[:, :, 0], ec, vr, ALU.mult, r=kE + [("bufA",)], w=[("vinit", 0)])
                tt("dve", vtmp[:, :, 0], es_, vi, ALU.mult, r=kE + [("bufC",)], w=[("vtmp", 0)])
                tt("dve", vinit[:, :, 1], ec, vi, ALU.mult, r=kE + [("bufC",)], w=[("vinit", 1)])
                tt("dve", vtmp[:, :, 1], es_, vr, ALU.mult, r=kE + [("bufA",)], w=[("vtmp", 1)])
                tt("dve", vinit[:, :, 0], vinit[:, :, 0], vtmp[:, :, 0], ALU.subtract, r=[("vinit", 0), ("vtmp", 0)], w=[("vinit", 0)])
                tt("dve", vinit[:, :, 1], vinit[:, :, 1], vtmp[:, :, 1], ALU.add, r=[("vinit", 1), ("vtmp", 1)], w=[("vinit", 1)])
            else:
                kS_ = [("Sst",)]
                nm1 = NCM - 1
                Vr, Vi = A3[:, :, 1:NCM], C3[:, :, 1:NCM]
                Ecm, Esm = Ec[:, :, 0:nm1], Es[:, :, 0:nm1]
                Bv = v3(bufB, nm1)
                Rv = v3(rtab, nm1)
                B2 = eang[:, :, 0:nm1]
                R2 = ekf[:, :, 0:nm1]
                tt("dve", Bv, Ecm, Vr, ALU.mult, r=kE + [("bufA",)], w=[("bufB",)])
                tt("dve", Rv, Esm, Vi, ALU.mult, r=kE + [("bufC",)], w=[("rtab",)])
                tt("dve", B2, Ecm, Vi, ALU.mult, r=kE + [("bufC",)], w=[("eang",)])
                tt("dve", R2, Esm, Vr, ALU.mult, r=kE + [("bufA",)], w=[("ekf",)])
                cp("dve", Sst[:, :, 0, 0], A3[:, :, 0], r=[("bufA",)], w=[("Sst", 0, 0)])
                cp("dve", Sst[:, :, 1, 0], C3[:, :, 0], r=[("bufC",)], w=[("Sst", 1, 0)])
                tt("dve", Sst[:, :, 0, 1:NCM], Bv, Rv, ALU.subtract, r=[("bufB",), ("rtab",)], w=[("Sst", 0, 1)])
                tt("dve", Sst[:, :, 1, 1:NCM], B2, R2, ALU.add, r=[("eang",), ("ekf",)], w=[("Sst", 1, 1)])

        if bt + 1 < 32 // PQ:
            emit_E(bt + 1)
            emit_Z(bt + 1)
        for pl in range(PQ):
            p = p0 + pl
            for g2 in range(2):
                g = 2 * p + g2
                gl = 2 * pl + g2
                if gl % 4 == 0:
                    ps, pk = psum()
                o = ps[:, 128 * (gl % 4):128 * (gl % 4) + 128]
                sl = slice(64 * g2, 64 * g2 + 64)
                mm(o, KW[:, g, :], u_scr[:, g, NCP:NCT], True, False, r=[("KW", g), ("u_scr", g)], w=[pk])
                mm(o, OW[sl, p, 0, :], Sst[sl, pl, 0, :], False, False, r=[("OW",)] + [("Sst", a_, b_) for a_ in range(2) for b_ in range(2)], w=[pk])
                mm(o, OW[sl, p, 1, :], Sst[sl, pl, 1, :], False, True, r=[("OW",)] + [("Sst", a_, b_) for a_ in range(2) for b_ in range(2)], w=[pk])
                if gl % 4 == 3:
                    act(z_scr[:, gl - 3:gl + 1, :].rearrange("p a b -> p (a b)"), ps[:, 0:512], AF.Gelu_apprx_tanh,
                        r=[pk], w=[("z_scr", gl // 4)])
        for bl in range(1):
            b = bt
            zv = z_fm[:, b, :].rearrange("p (m j) -> p j m", j=8)
            for jh in range(2):
                ps, pk = psum()
                for jj in range(4):
                    j = 4 * jh + jj
                    for gp in range(8):
                        mm(ps[:, 128 * jj:128 * jj + 128], sel[:, j * 8 + gp, :], z_scr[:, 8 * bl + gp, :], gp == 0, gp == 7,
                           r=[("sel",)] + [("z_scr", k) for k in range(2)], w=[pk])
                cp("act", zv[:, 4 * jh:4 * jh + 4, :], ps[:, 0:512].rearrange("p (a b) -> p a b", a=4),
                   r=[pk], w=[("z_fm", b)])
    dump("z_fm", z_fm, [("z_fm", b) for b in range(8)])
    dump("u_scr", u_scr, [("u_scr", g) for g in range(64)])
    M.free("u_scr", "Sst", "z_scr", "Ec0", "Ec1", "Es0", "Es1", "eang", "ekf", "eki", "Zs0", "Zs1", "bufA", "bufB", "bufC", "rtab", "vinit", "vtmp")
    M.free("ZW", "KW", "OW", "rho", "fcos", "fsin", "dsk", "sel", "phiw", "qv")

    alloc_wslots(6)
    cbuf = M.alloc("cbuf", [128, 8, 30 + NM], BF16)
    sigT = M.alloc("sigT", [128, 2, 30 + NM], BF16)
    cv_rl = main_rl + [(lambda kc: hb_halo[:, kc, :], 30)]
    cv_cols = ((30, 512), (30 + 512, 512), (0, 30))
    for jj in range(4):
        wt2, wk2 = wtile(w_in, 0, 16, 2048 + 256 * jj)
        wt1, wk1 = wtile(w_in, 0, 16, 1024 + 256 * jj)
        for sub in range(2):
            outs = proj(wt2, wk2, 16, 128 * sub, cv_rl, khb + [("hb_halo",)])
            for (a, n), (ps, pk, _) in zip(cv_cols, outs):
                act(sigT[:, sub, a:a + n], ps[:, 0:n], AF.Sigmoid, r=[pk], w=[("sigT", sub, a)])
        for sub in range(2):
            j = 2 * jj + sub
            outs = proj(wt1, wk1, 16, 128 * sub, cv_rl, khb + [("hb_halo",)])
            for (a, n), (ps, pk, _) in zip(cv_cols, outs):
                tt("dve", cbuf[:, j, a:a + n], ps[:, 0:n], sigT[:, sub, a:a + n], ALU.mult,
                   r=[pk, ("sigT", sub, a)], w=[("cbuf", j)])
    M.free("sigT", "hb_halo")
    dump("cbuf", cbuf, [("cbuf", j) for j in range(8)])

    merged = M.alloc("merged", [128, 16, NM], BF16)
    g1 = [M.alloc(f"g1_{i}", [128, 512], F32) for i in range(2)]
    g2b = [M.alloc(f"g2_{i}", [128, 512], F32) for i in range(2)]
    g_rr = [0]
    kz = [("z_fm", b) for b in range(8)]
    z_rl = [(lambda kc, a=a: z_fm[:, kc, a:a + 512], 512) for a in (0, 512)]
    for mm_ in range(8):
        wtA, wkA = wtile(w_glu, 0, 8, 256 * mm_)
        wtB, wkB = wtile(w_glu, 0, 8, 2048 + 256 * mm_)
        wtG, wkG = wtile(w_in, 0, 16, 3072 + 256 * mm_)
        for sub in range(2):
            mc = 2 * mm_ + sub
            oB = proj(wtB, wkB, 8, 128 * sub, z_rl, kz)
            oG = proj(wtG, wkG, 16, 128 * sub, main_rl, khb)
            oA = proj(wtA, wkA, 8, 128 * sub, z_rl, kz)
            for nt in range(2):
                i = g_rr[0] % 2
                g_rr[0] += 1
                a = 512 * nt
                act(g1[i], oB[nt][0][:, 0:512], AF.Sigmoid, r=[oB[nt][1]], w=[(f"g1_{i}",)])
                act(g2b[i], oG[nt][0][:, 0:512], AF.Sigmoid, r=[oG[nt][1]], w=[(f"g2_{i}",)])
                tt("dve", g1[i], g1[i], g2b[i], ALU.mult, r=[(f"g1_{i}",), (f"g2_{i}",)], w=[(f"g1_{i}",)])
                tt("dve", merged[:, mc, a:a + 512], oA[nt][0][:, 0:512], g1[i], ALU.mult,
                   r=[oA[nt][1], (f"g1_{i}",)], w=[("merged", mc, nt)])
    M.free("z_fm")
    free_wslots()
    alloc_wslots(4)
    dump("merged_a", merged, [("merged", mc, nt) for mc in range(16) for nt in range(2)])

    convout = M.alloc("convout", [128, 8, NM], F32)
    diag = [M.alloc(f"diag{i}", [128, 31, 128], BF16) for i in range(2)]
    sqb = [M.alloc(f"sqb{i}", [128, NM], BF16) for i in range(2)]
    cbb = [M.alloc(f"cbb{i}", [128, NM], BF16) for i in range(2)]
    conv_banks, stat_banks = [0, 1, 2, 3], [4, 5, 6, 7]
    cb_ctr = [0]
    st_ps = [psum(bk) for bk in stat_banks]
    for j in range(8):
        di = j % 2
        for k in range(31):
            ts("dve", diag[di][:, k, :], identf, convw[:, j, k:k + 1], None, ALU.mult, None,
               r=[("identf",), ("convw",)], w=[(f"diag{di}", k)])
        for nt in range(2):
            ps, pk = psum_from(conv_banks, cb_ctr)
            for k in range(31):
                mm(ps[:, 0:512], diag[di][:, k, :], cbuf[:, j, 512 * nt + k:512 * nt + k + 512], k == 0, k == 30,
                   r=[(f"diag{di}", k), ("cbuf", j)], w=[pk])
            act(convout[:, j, 512 * nt:512 * nt + 512], ps[:, 0:512], AF.Identity, r=[pk], w=[("convout", j, nt)],
                bias=convb[:, j:j + 1])
        kco = [("convout", j, 0), ("convout", j, 1)]
        act(sqb[di], convout[:, j, :], AF.Square, r=kco, w=[(f"sqb{di}",)])
        cp("dve", cbb[di], convout[:, j, :], r=kco, w=[(f"cbb{di}",)])
        for nt in range(2):
            mm(st_ps[nt][0][:, 0:512], onesb, cbb[di][:, 512 * nt:512 * nt + 512], j == 0, j == 7,
               r=[("onesb",), (f"cbb{di}",)], w=[st_ps[nt][1]])
            mm(st_ps[2 + nt][0][:, 0:512], onesb, sqb[di][:, 512 * nt:512 * nt + 512], j == 0, j == 7,
               r=[("onesb",), (f"sqb{di}",)], w=[st_ps[2 + nt][1]])
    M.free("diag0", "diag1", "sqb0", "sqb1", "cbb0", "cbb1", "cbuf")
    mean_t = M.alloc("mean_t", [128, NM], F32)
    rstd_t = M.alloc("rstd_t", [128, NM], F32)
    msq_t = M.alloc("msq_t", [128, 512], F32)
    for nt in range(2):
        sl = slice(512 * nt, 512 * nt + 512)
        ts("dve", mean_t[:, sl], st_ps[nt][0][:, 0:512], 1.0 / 1024, None, ALU.mult, None, r=[st_ps[nt][1]], w=[("mean_t", nt)])
        tt("dve", msq_t, mean_t[:, sl], mean_t[:, sl], ALU.mult, r=[("mean_t", nt)], w=[("msq_t",)])
        stt(rstd_t[:, sl], st_ps[2 + nt][0][:, 0:512], 1.0 / 1024, msq_t, ALU.mult, ALU.subtract,
            r=[st_ps[2 + nt][1], ("msq_t",)], w=[("rstd_t", nt)])
        ts("dve", rstd_t[:, sl], rstd_t[:, sl], 0.0, None, ALU.max, None, r=[("rstd_t", nt)], w=[("rstd_t", nt)])
        act(rstd_t[:, sl], rstd_t[:, sl], AF.Sqrt, r=[("rstd_t", nt)], w=[("rstd_t", nt)], scale=1.0, bias=EPS)
        S.op("dve", lambda e, sl=sl: e.reciprocal(out=rstd_t[:, sl], in_=rstd_t[:, sl]), r=[("rstd_t", nt)], w=[("rstd_t", nt)])
    csb = M.alloc("csb", [128, 8, NM], BF16)
    kst = [("mean_t", 0), ("mean_t", 1), ("rstd_t", 0), ("rstd_t", 1)]
    for j in range(8):
        kco = [("convout", j, 0), ("convout", j, 1)]
        tt("dve", convout[:, j, :], convout[:, j, :], mean_t, ALU.subtract, r=kco + kst, w=kco)
        tt("dve", convout[:, j, :], convout[:, j, :], rstd_t, ALU.mult, r=kco + kst, w=kco)
        act(csb[:, j, :], convout[:, j, :], AF.Silu, r=kco + [("lng",), ("lnb",)], w=[("csb", j)],
            scale=lng[:, j:j + 1], bias=lnb[:, j:j + 1])
    dump("csb", csb, [("csb", j) for j in range(8)])
    M.free("convout", "mean_t", "rstd_t", "msq_t")
    kcs = [("csb", j) for j in range(8)]
    cs_rl = [(lambda kc, a=a: csb[:, kc, a:a + 512], 512) for a in (0, 512)]
    for mm_ in range(8):
        wtW, wkW = wtile(w_co, 0, 8, 256 * mm_)
        wtG, wkG = wtile(w_in, 0, 16, 5120 + 256 * mm_)
        for sub in range(2):
            mc = 2 * mm_ + sub
            oG = proj(wtG, wkG, 16, 128 * sub, main_rl, khb)
            oW = proj(wtW, wkW, 8, 128 * sub, cs_rl, lambda kc, ri: [("csb", kc)])
            for nt in range(2):
                i = g_rr[0] % 2
                g_rr[0] += 1
                a = 512 * nt
                act(g2b[i], oG[nt][0][:, 0:512], AF.Sigmoid, r=[oG[nt][1]], w=[(f"g2_{i}",)])
                tt("dve", g1[i], oW[nt][0][:, 0:512], g2b[i], ALU.mult, r=[oW[nt][1], (f"g2_{i}",)], w=[(f"g1_{i}",)])
                tt("dve", merged[:, mc, a:a + 512], merged[:, mc, a:a + 512], g1[i], ALU.add,
                   r=[("merged", mc, nt), (f"g1_{i}",)], w=[("merged", mc, nt)])
    M.free("csb", "hb", "g1_0", "g1_1", "g2_0", "g2_1")

    dump("merged", merged, [("merged", mc, nt) for mc in range(16) for nt in range(2)])
    xres = M.alloc("xres", [128, 16, NM], F32)
    alloc_load_bufs()
    load_tokens(xm_d, NM, None, None, xres=xres)
    free_load_bufs()
    kmg = [("merged", mc, nt) for mc in range(16) for nt in range(2)]
    mg_rl = [(lambda kc, a=a: merged[:, kc, a:a + 512], 512) for a in (0, 512)]
    for mm_ in range(8):
        wt, wk = wtile(w_out, 0, 16, 256 * mm_)
        for sub in range(2):
            mc = 2 * mm_ + sub
            outs = proj(wt, wk, 16, 128 * sub, mg_rl, lambda kc, ri: [("merged", kc, ri)])
            for nt, (ps, pk, _) in enumerate(outs):
                sl = slice(512 * nt, 512 * nt + 512)
                tt("dve", xres[:, mc, sl], xres[:, mc, sl], ps[:, 0:512], ALU.add, r=[pk, ("xres", nt)], w=[("xres", nt)])
    M.free("merged")
    dump("xres1", xres, [("xres", 0), ("xres", 1)])

    hb2 = M.alloc("hb2", [128, 16, NM], BF16)
    sq2 = [M.alloc(f"sq2_{i}", [128, 512], BF16) for i in range(2)]
    rstd2 = M.alloc("rstd2", [128, NM], F32)
    for nt in range(2):
        sl = slice(512 * nt, 512 * nt + 512)
        ps, pk = psum()
        for kc in range(16):
            i = kc % 2
            act(sq2[i], xres[:, kc, sl], AF.Square, r=[("xres", nt)], w=[(f"sq2_{i}",)])
            mm(ps[:, 0:512], onesb, sq2[i], kc == 0, kc == 15, r=[("onesb",), (f"sq2_{i}",)], w=[pk])
        act(rstd2[:, sl], ps[:, 0:512], AF.Sqrt, r=[pk], w=[("rstd2", nt)], scale=1.0 / D, bias=EPS)
        S.op("dve", lambda e, sl=sl: e.reciprocal(out=rstd2[:, sl], in_=rstd2[:, sl]), r=[("rstd2", nt)], w=[("rstd2", nt)])
        for kc in range(16):
            stt(hb2[:, kc, sl], xres[:, kc, sl], gffn[:, kc:kc + 1], rstd2[:, sl], ALU.mult, ALU.mult,
                r=[("xres", nt), ("gffn",), ("rstd2", nt)], w=[("hb2", "f", nt)])
    kh2 = [("hb2", "f", 0), ("hb2", "f", 1)]
    h2_rl = [(lambda kc, a=a: hb2[:, kc, a:a + 512], 512) for a in (0, 512)]

    M.free("rstd2")
    hid = [M.alloc("hid0", [128, 16, NM], BF16)]
    rl_t = [M.alloc(f"rl_t{i}", [128, 512], F32) for i in range(2)]
    r_rr = [0]
    for hbk in range(4):
        hd = hid[0]
        hn = "hid0"
        for mm_ in range(8):
            wt, wk = wtile(w_ff1, 0, 16, 2048 * hbk + 256 * mm_)
            for sub in range(2):
                mc = 2 * mm_ + sub
                outs = proj(wt, wk, 16, 128 * sub, h2_rl, kh2)
                for nt, (ps, pk, _) in enumerate(outs):
                    i = r_rr[0] % 2
                    r_rr[0] += 1
                    act(rl_t[i], ps[:, 0:512], AF.Relu, r=[pk], w=[(f"rl_t{i}",)])
                    tt("dve", hd[:, mc, 512 * nt:512 * nt + 512], rl_t[i], rl_t[i], ALU.mult,
                       r=[(f"rl_t{i}",)], w=[(hn, mc, nt)])
        khid = [(hn, mc, nt) for mc in range(16) for nt in range(2)]
        hd_rl = [(lambda kc, a=a, hd=hd: hd[:, kc, a:a + 512], 512) for a in (0, 512)]
        for mm_ in range(8):
            wt, wk = wtile(w_ff2, 2048 * hbk, 16, 256 * mm_)
            for sub in range(2):
                mc = 2 * mm_ + sub
                outs = proj(wt, wk, 16, 128 * sub, hd_rl, lambda kc, ri, hn=hn: [(hn, kc, ri)])
                for nt, (ps, pk, _) in enumerate(outs):
                    sl = slice(512 * nt, 512 * nt + 512)
                    tt("dve", xres[:, mc, sl], xres[:, mc, sl], ps[:, 0:512], ALU.add, r=[pk, ("xres", nt)], w=[("xres", nt)])
    M.free("hid0", "rl_t0", "rl_t1", "hb2", "sq2_0", "sq2_1")

    dump("xres2", xres, [("xres", 0), ("xres", 1)])
    gf_bc = M.alloc("gf_bc", [128, D], F32)
    dma("sp", gf_bc, g_f_d.partition_broadcast(128), r=(), w=[("gf_bc",)])
    junkl.clear()
    junkl.append(M.alloc("junk", [128, 2, 512], BF16))
    yo = [M.alloc(f"yo{i}", [128, D], F32) for i in range(2)]
    fss = M.alloc("fss", [128, 2, 4], F32)
    fs1 = M.alloc("fs1", [128, 2], F32)
    for tt_ in range(8):
        i = tt_ % 2
        nt = tt_ // 4
        pss = []
        for q in range(4):
            ps, pk = psum(4 * i + q)
            pss.append((ps, pk))
            for kk in range(4):
                kc = 4 * q + kk
                tr(ps[:, 128 * kk:128 * kk + 128], xres[:, kc, 128 * tt_:128 * tt_ + 128], identf,
                   r=[("xres", nt), ("identf",)], w=[pk])
            act(junkl[0][:, q % 2, :], ps[:, 0:512], AF.Square, r=[pk], w=[("junk", q % 2), ("fss", i, q)],
                accum=fss[:, i, q:q + 1])
        kf = [("fss", i, q) for q in range(4)]
        S.op("dve", lambda e, i=i: e.reduce_sum(out=fs1[:, i:i + 1], in_=fss[:, i, :], axis=mybir.AxisListType.X),
             r=kf, w=[("fs1", i)])
        act(fs1[:, i:i + 1], fs1[:, i:i + 1], AF.Sqrt, r=[("fs1", i)], w=[("fs1", i)], scale=1.0 / D, bias=EPS)
        S.op("dve", lambda e, i=i: e.reciprocal(out=fs1[:, i:i + 1], in_=fs1[:, i:i + 1]), r=[("fs1", i)], w=[("fs1", i)])
        for q in range(4):
            ps, pk = pss[q]
            stt(yo[i][:, 512 * q:512 * q + 512], ps[:, 0:512], fs1[:, i:i + 1], gf_bc[:, 512 * q:512 * q + 512],
                ALU.mult, ALU.mult, r=[pk, ("fs1", i), ("gf_bc",)], w=[(f"yo{i}", q)])
        dma("sp", y_d[128 * tt_:128 * tt_ + 128, :], yo[i], r=[(f"yo{i}", q) for q in range(4)], w=[("y_d", tt_)],
            group="y", slot=i)

    S.finalize()
    bad = sorted({k[0] for k in S.tiles if isinstance(k, tuple)} - M.all_names - {"psum", "dbg", "y_d"})
    assert not bad, f"tile keys without buffer: {bad}"
    sems = {e: es.enter_context(nc.semaphore("s_" + e)) for e in Sched.ENGS}
    dsems = [es.enter_context(nc.semaphore(f"d_{i}")) for i in range(S.n_dma_sems)]
    with nc.Block() as block:
        @block.tensor
        def _(e):
            S.emit("pe", e, sems, dsems)

        @block.scalar
        def _(e):
            S.emit("act", e, sems, dsems)

        @block.vector
        def _(e):
            S.emit("dve", e, sems, dsems)

        @block.gpsimd
        def _(e):
            S.emit("pool", e, sems, dsems)

        @block.sync
        def _(e):
            S.emit("sp", e, sems, dsems, final_wait_all_dma=True)
    es.close()
    return nc


def _consts():
    ident = np.eye(128, dtype=np.float32)
    sel = np.zeros((128, 64, 128), dtype=np.float32)
    for a in range(8):
        for b in range(8):
            for ch in range(16):
                sel[16 * a + ch, a * 8 + b, 16 * b + ch] = 1.0
    mask = np.zeros((128, 128), dtype=np.float32)
    for i in range(8):
        for j in range(i, 8):
            mask[16 * i:16 * i + 16, 16 * j:16 * j + 16] = 1.0
    kvec = np.array([7, 6, 5, 4, 3, 2, 1, 0, 1, 2, 3, 4, 5, 6, 7, 8, -7, -6, -5, -4, -3, -2, -1, 0], dtype=np.float32)
    qvec = np.arange(1, NCP + 1, dtype=np.float32)
    return {"c_ident": ident, "c_sel": sel.reshape(128, 64 * 128), "c_mask": mask, "c_kvec": kvec, "c_qvec": qvec}


_NC_CACHE = {}


def kernel(**inputs):
    f = lambda a: np.ascontiguousarray(np.asarray(a, dtype=np.float32))
    x = f(inputs["x"])
    meta = f(inputs["meta"])
    shared = {
        "w_in": f(inputs["w_in"])[0], "w_glu": f(inputs["w_glu"])[0], "w_conv_out": f(inputs["w_conv_out"])[0],
        "w_out": f(inputs["w_out"])[0], "w_ff1": f(inputs["w_ff1"])[0], "w_ff2": f(inputs["w_ff2"])[0],
        "lam_re": f(inputs["lam_re"])[0], "lam_im": f(inputs["lam_im"])[0], "log_dt": f(inputs["log_dt"])[0],
        "b_re": f(inputs["b_re"])[0], "b_im": f(inputs["b_im"])[0], "c_re": f(inputs["c_re"])[0], "c_im": f(inputs["c_im"])[0],
        "d_skip": f(inputs["d_skip"])[0], "conv_w": f(inputs["conv_w"])[0], "conv_b": f(inputs["conv_b"])[0],
        "conv_ln_g": f(inputs["conv_ln_g"])[0], "conv_ln_b": f(inputs["conv_ln_b"])[0],
        "norm_mix_g": f(inputs["norm_mix_g"])[0], "norm_ffn_g": f(inputs["norm_ffn_g"])[0], "norm_f_g": f(inputs["norm_f_g"]),
    }
    shared.update(_consts())
    B = x.shape[0]
    in_maps = []
    for b in range(B):
        for h in range(2):
            if h == 0:
                xp = np.concatenate([np.zeros((NP - 16, D), np.float32), meta], axis=0)
                xm = x[b, 0:NM]
            else:
                xp = np.concatenate([meta, x[b, 0:NM]], axis=0)
                xm = x[b, NM:2 * NM]
            m = dict(shared)
            m["xp"] = np.ascontiguousarray(xp)
            m["xm"] = np.ascontiguousarray(xm)
            in_maps.append(m)
    if "nc" not in _NC_CACHE:
        _NC_CACHE["nc"] = build_program()
    nc = _NC_CACHE["nc"]
    res = run_bass_kernel_spmd(nc, in_maps, core_ids=list(range(8)))
    if DEBUG:
        _NC_CACHE["res"] = res
    out = np.empty((B, 2 * NM, D), dtype=np.float32)
    for b in range(B):
        for h in range(2):
            out[b, h * NM:(h + 1) * NM] = res.results[2 * b + h]["y"]
    return out
```
